# Optimizing a Trainium2 kernel written in Bass

```python
import jax, jax.numpy as jnp
from jax import lax
import numpy as np

D_MODEL = 1024
BATCH = 8
SEQ = 8192
DEPTH = 1

D_MIX = D_MODEL
D_A = D_MIX // 2
D_B = D_MIX - D_A
HEAD_DIM = 64
N_GROUPS_A = D_A // HEAD_DIM
N_GROUPS_B = D_B // HEAD_DIM
K_SHORT = 3
K_CONFORMER = 31
K_FFN = 3
D_FF = 2816
D_IN = 3 * D_A + 2 * D_B
RMS_EPS = 1e-6
LN_EPS = 1e-5

kernel_name = "hybrid_shortconv_conformer_convffn_block"


def rmsnorm(x, g):
    xf = x.astype(jnp.float32)
    y = xf * lax.rsqrt(jnp.mean(xf * xf, axis=-1, keepdims=True) + RMS_EPS)
    return (y * g.astype(jnp.float32)).astype(x.dtype)


def layernorm(x, g, b):
    xf = x.astype(jnp.float32)
    mu = jnp.mean(xf, axis=-1, keepdims=True)
    var = jnp.mean(jnp.square(xf - mu), axis=-1, keepdims=True)
    y = (xf - mu) * lax.rsqrt(var + LN_EPS)
    return (y * g.astype(jnp.float32) + b.astype(jnp.float32)).astype(x.dtype)


def dwconv(x, w):
    k, c = w.shape
    rhs = w[:, None, :].astype(x.dtype)
    return lax.conv_general_dilated(
        x, rhs, window_strides=(1,), padding=[(k // 2, k // 2)],
        dimension_numbers=("NWC", "WIO", "NWC"), feature_group_count=c)


def setup_inputs(seed: int = 0) -> dict:
    key = jax.random.key(seed)
    ks = jax.random.split(key, 16)
    f32 = jnp.float32

    def nrm(k, shape, scale):
        return jax.random.normal(k, shape, f32) * scale

    return {
        "x": jax.random.normal(ks[0], (BATCH, SEQ, D_MODEL), f32),
        "norm_mix_g": 1.0 + nrm(ks[1], (DEPTH, D_MODEL), 0.05),
        "w_in": nrm(ks[2], (DEPTH, D_MODEL, D_IN), D_MODEL ** -0.5),
        "conv_a_w": nrm(ks[3], (DEPTH, K_SHORT, D_A), K_SHORT ** -0.5),
        "conv_b_w": nrm(ks[4], (DEPTH, K_CONFORMER, D_B), K_CONFORMER ** -0.5),
        "conv_b_b": nrm(ks[5], (DEPTH, D_B), 0.02),
        "ln_b_g": 1.0 + nrm(ks[6], (DEPTH, D_B), 0.05),
        "ln_b_b": nrm(ks[7], (DEPTH, D_B), 0.02),
        "w_out": nrm(ks[8], (DEPTH, D_MIX, D_MODEL), D_MIX ** -0.5),
        "norm_ffn_g": 1.0 + nrm(ks[9], (DEPTH, D_MODEL), 0.05),
        "w_gate": nrm(ks[10], (DEPTH, D_MODEL, D_FF), D_MODEL ** -0.5),
        "w_up": nrm(ks[11], (DEPTH, D_MODEL, D_FF), D_MODEL ** -0.5),
        "conv_ffn_w": nrm(ks[12], (DEPTH, K_FFN, D_FF), K_FFN ** -0.5),
        "w_down": nrm(ks[13], (DEPTH, D_FF, D_MODEL), D_FF ** -0.5),
        "norm_final_g": 1.0 + nrm(ks[14], (D_MODEL,), 0.05),
    }


def reference(x, norm_mix_g, w_in, conv_a_w, conv_b_w, conv_b_b, ln_b_g, ln_b_b,
              w_out, norm_ffn_g, w_gate, w_up, conv_ffn_w, w_down, norm_final_g):
    for l in range(DEPTH):
        h = rmsnorm(x, norm_mix_g[l])
        z = jnp.einsum("bsd,de->bse", h, w_in[l])
        a_h, a_bg, a_cg, b_val, b_gate = jnp.split(
            z, [D_A, 2 * D_A, 3 * D_A, 3 * D_A + D_B], axis=-1)
        y_a = a_bg * dwconv(a_cg * a_h, conv_a_w[l])
        u = b_val * jax.nn.sigmoid(b_gate)
        u = dwconv(u, conv_b_w[l]) + conv_b_b[l].astype(u.dtype)
        y_b = jax.nn.silu(layernorm(u, ln_b_g[l], ln_b_b[l]))
        y = jnp.concatenate([y_a, y_b], axis=-1)
        x = x + jnp.einsum("bse,ed->bsd", y, w_out[l])

        h = rmsnorm(x, norm_ffn_g[l])
        g = dwconv(jnp.einsum("bsd,df->bsf", h, w_gate[l]), conv_ffn_w[l])
        v = jnp.einsum("bsd,df->bsf", h, w_up[l])
        x = x + jnp.einsum("bsf,fd->bsd", jax.nn.silu(g) * v, w_down[l])

    return rmsnorm(x, norm_final_g)
```

```python
import numpy as np
from contextlib import ExitStack

import concourse.bass as bass
import concourse.mybir as mybir
from concourse.bass_utils import run_bass_kernel_spmd

F32 = mybir.dt.float32
BF16 = mybir.dt.bfloat16
AF = mybir.ActivationFunctionType
ALU = mybir.AluOpType

D = 1024
DA = 512
DB = 512
DIN = 2560
DFF = 2816
NF = DFF // 128
KC = D // 128
NE = DIN // 128
RMS_EPS = 1e-6
LN_EPS = 1e-5
HALO = 16
WIN = 512

A_ORDER = [16, 12, 17, 13, 18, 14, 19, 15, 0, 8, 1, 9, 2, 10, 3, 11, 4, 5, 6, 7]

N_WIN_SLOTS = 5
N_GU_SLOTS = 3
N_DIAG_BLK = 3
DIAG_ENGINE = "dve"
LN_ENGINE = "pool"


def plan_tiles(S):
    tiles = []
    t0 = 0
    while t0 < S:
        if t0 == 0:
            T = min(S, WIN - HALO)
        else:
            T = min(WIN - 2 * HALO, S - t0)
            if S - t0 <= WIN - HALO:
                T = S - t0
        tiles.append((t0, T))
        t0 += T
    return tiles


class Prog:
    COMPUTE = ("act", "dve", "pool", "pe")

    def __init__(self):
        self.streams = {e: [] for e in ("sp", "act", "dve", "pool", "pe")}
        self.ccount = {e: 0 for e in self.COMPUTE}
        self.dcount = {}
        self.state = {}
        self.waited = {e: {} for e in self.streams}
        self.semnames = set("c_" + e for e in self.COMPUTE)
        self.final_tokens = []

    def _deps(self, eng, reads, writes, is_dma):
        need = {}

        def add(tok, raw):
            sem, val, src = tok
            if src == eng and not is_dma:
                if eng == "pe":
                    return
                if not raw:
                    return
            if need.get(sem, 0) < val:
                need[sem] = val

        for c in reads:
            st = self.state.get(c)
            if st is not None and st[0] is not None:
                add(st[0], True)
        for c in writes:
            st = self.state.get(c)
            if st is not None:
                if st[0] is not None:
                    add(st[0], False)
                for t in st[1].values():
                    add(t, False)
        waits = []
        wd = self.waited[eng]
        for sem, val in need.items():
            if wd.get(sem, 0) < val:
                wd[sem] = val
                waits.append((sem, val))
        return waits

    def _commit(self, tok, reads, writes):
        for c in reads:
            st = self.state.get(c)
            if st is None:
                st = [None, {}]
                self.state[c] = st
            st[1][(tok[0], tok[2])] = tok
        for c in writes:
            self.state[c] = [tok, {}]

    def op(self, eng, fn, reads=(), writes=()):
        reads = tuple(reads)
        writes = tuple(writes)
        waits = self._deps(eng, reads, writes, False)
        self.ccount[eng] += 1
        tok = ("c_" + eng, self.ccount[eng], eng)
        self.streams[eng].append((waits, fn, "c_" + eng, 1))
        self._commit(tok, reads, writes)
        return tok

    def dma(self, queue, fn, n, sem, reads=(), writes=()):
        reads = tuple(reads)
        writes = tuple(writes)
        waits = self._deps(queue, reads, writes, True)
        self.semnames.add(sem)
        self.dcount[sem] = self.dcount.get(sem, 0) + n
        tok = (sem, 16 * self.dcount[sem], queue)
        self.streams[queue].append((waits, fn, sem, 16))
        self._commit(tok, reads, writes)
        return tok

    def emit(self, nc, es):
        sems = {name: es.enter_context(nc.semaphore(name)) for name in sorted(self.semnames)}
        block = es.enter_context(nc.Block())
        streams = self.streams
        finals = self.final_tokens

        awaited = set()
        for name in streams:
            for waits, fn, sem, inc in streams[name]:
                for s, v in waits:
                    awaited.add((s, v))
        remap = {}
        for name in self.COMPUTE:
            sem = "c_" + name
            new = 0
            idx = 0
            m = {}
            for waits, fn, s_, inc in streams[name]:
                if inc != 1:
                    continue
                idx += 1
                if (sem, idx) in awaited:
                    new += 1
                    m[idx] = new
            remap[sem] = m

        def run(eng, name):
            idx = 0
            for waits, fn, sem, inc in streams[name]:
                for s, v in waits:
                    if s in remap:
                        v = remap[s][v]
                    eng.wait_ge(sems[s], v)
                res = fn(eng)
                if inc == 16:
                    for ins in res:
                        ins.then_inc(sems[sem], 16)
                else:
                    idx += 1
                    if idx in remap[sem]:
                        last = res[-1] if isinstance(res, (list, tuple)) else res
                        last.then_inc(sems[sem], 1)
            if name == "sp":
                done = {}
                for s, v, _ in finals:
                    done[s] = max(done.get(s, 0), v)
                for s, v in done.items():
                    eng.wait_ge(sems[s], v)

        @block.sync
        def _(e):
            run(e, "sp")

        @block.scalar
        def _(e):
            run(e, "act")

        @block.vector
        def _(e):
            run(e, "dve")

        @block.gpsimd
        def _(e):
            run(e, "pool")

        @block.tensor
        def _(e):
            run(e, "pe")


def arcells(lo, hi):
    return [("ar", i) for i in range(lo // 1024, (hi - 1) // 1024 + 1)]


def build(S):
    nc = bass.Bass("TRN2", target_bir_lowering=False)
    dt = nc.dram_tensor
    x = dt("x", [S, D], F32, kind="ExternalInput").ap()
    norm_mix_g = dt("norm_mix_g", [D], F32, kind="ExternalInput").ap()
    w_in = dt("w_in", [D, DIN], F32, kind="ExternalInput").ap()
    conv_a_w = dt("conv_a_w", [3, DA], F32, kind="ExternalInput").ap()
    conv_b_w = dt("conv_b_w", [31, DB], F32, kind="ExternalInput").ap()
    conv_b_b = dt("conv_b_b", [DB], F32, kind="ExternalInput").ap()
    ln_b_g = dt("ln_b_g", [DB], F32, kind="ExternalInput").ap()
    ln_b_b = dt("ln_b_b", [DB], F32, kind="ExternalInput").ap()
    w_out = dt("w_out", [D, D], F32, kind="ExternalInput").ap()
    norm_ffn_g = dt("norm_ffn_g", [D], F32, kind="ExternalInput").ap()
    w_gate = dt("w_gate", [D, DFF], F32, kind="ExternalInput").ap()
    w_up = dt("w_up", [D, DFF], F32, kind="ExternalInput").ap()
    conv_ffn_w = dt("conv_ffn_w", [3, DFF], F32, kind="ExternalInput").ap()
    w_down = dt("w_down", [DFF, D], F32, kind="ExternalInput").ap()
    norm_final_g = dt("norm_final_g", [D], F32, kind="ExternalInput").ap()
    y = dt("y", [S, D], F32, kind="ExternalOutput").ap()
    scr_in = dt("scr_in", [NE, 128, 1024], BF16, kind="Internal").ap()
    scr_gu = dt("scr_gu", [NF, 128, 2, 1024], BF16, kind="Internal").ap()

    tiles = plan_tiles(S)
    P = Prog()

    with ExitStack() as es:
        def sb(name, shape, dtype):
            return es.enter_context(nc.sbuf_tensor(name, shape, dtype))

        wout = sb("wout", [128, KC, D], BF16)
        wdn = sb("wdn", [128, NF, D], BF16)
        winr = sb("winr", [128, N_WIN_SLOTS, 1024], BF16)
        gur = sb("gur", [128, N_GU_SLOTS, 2, 1024], BF16)
        diag = sb("diag", [128, N_DIAG_BLK, 8, 128], BF16)
        gfin = sb("gfin", [128, D], F32)
        identf = sb("identf", [128, 128], F32)
        identh = sb("identh", [128, 128], F32)
        identb = sb("identb", [128, 128], BF16)
        onesb = sb("onesb", [128, 128], BF16)
        gmix = sb("gmix", [128, KC], F32)
        gffn = sb("gffn", [128, KC], F32)
        caw = sb("caw", [128, 4, 3], F32)
        cbw = sb("cbw", [128, 4, 31], F32)
        cbb = sb("cbb", [128, 4], F32)
        lng = sb("lng", [128, 4], F32)
        lnb = sb("lnb", [128, 4], F32)
        cfw = sb("cfw", [128, NF, 3], F32)
        epsc = sb("epsc", [128, 2], F32)
        xa = sb("xa", [128, 2, D], F32)
        hbf = sb("hbf", [128, 3, D], BF16)
        hbs = sb("hbs", [128, 4, D], BF16)
        hT = sb("hT", [128, KC, WIN], BF16)
        h2T = sb("h2T", [128, KC, WIN], BF16)
        yT = sb("yT", [128, KC, WIN], BF16)
        x1 = sb("x1", [128, 4, D], F32)
        NSTAT = 16
        stat = sb("stat", [128, NSTAT, 4], F32)
        ev = sb("ev", [128, 3, WIN], F32)
        UPW = 544
        upad = sb("upad", [128, 4, UPW], BF16)
        PPW = 516
        pp = sb("pp", [128, 2, PPW], F32)
        ct = sb("ct", [128, 4, WIN], F32)
        ARENA_E = 13312
        arena = sb("arena", [128, ARENA_E], BF16)

        pb = [es.enter_context(nc.psum_tensor("pb%d" % i, [128, 512], F32)) for i in range(6)]
        tp = [es.enter_context(nc.psum_tensor("tp%d" % i, [128, KC, 128], BF16)) for i in range(2)]

        def PS(b):
            return ("ps", b)

        stat_ps = [tp[b][:, :, :].rearrange("p k c -> p (k c)").bitcast(F32) for b in range(2)]

        def pbv(b):
            return pb[b] if b < 6 else stat_ps[b - 6]

        def ar_bf(lo_b, n):
            return arena[:, lo_b // 2: lo_b // 2 + n]

        def ar_f32(lo_b, n):
            return arena[:, lo_b // 2: lo_b // 2 + 2 * n].bitcast(F32)

        def aT(j):
            return ar_bf(j * 1024, 512)

        def aT_cells(j):
            return arcells(j * 1024, (j + 1) * 1024)

        def c1v(s):
            return ar_f32(22528 + s * 2048, 512)

        def c1_cells(s):
            return arcells(22528 + s * 2048, 22528 + (s + 1) * 2048)

        uc_all = ar_f32(0, 2048)

        def ucv(c):
            return uc_all[:, c * 512:(c + 1) * 512]

        def uc_cells(c):
            return arcells(c * 2048, (c + 1) * 2048)

        def ucbv(c):
            return ar_bf(8192 + c * 1024, 512)

        def ucb_cells(c):
            return arcells(8192 + c * 1024, 8192 + (c + 1) * 1024)

        def usqv(c):
            return ar_bf(12288 + c * 1024, 512)

        def usq_cells(c):
            return arcells(12288 + c * 1024, 12288 + (c + 1) * 1024)

        mean_sb = ar_f32(16384, 512)
        mean_cells = arcells(16384, 18432)
        var_sb = ar_f32(18432, 512)
        var_cells = arcells(18432, 20480)
        mr_sb = ar_f32(20480, 512)
        mr_cells = arcells(20480, 22528)

        def tlnv(s):
            return c1v(s)

        def tln_cells(s):
            return c1_cells(s)

        def stgv(s):
            return ar_f32(s * 4096, 1024)

        def stg_cells(s):
            return arcells(s * 4096, (s + 1) * 4096)

        def obv(s):
            return ar_bf(8192 + s * 2048, 1024)

        def ob_cells(s):
            return arcells(8192 + s * 2048, 8192 + (s + 1) * 2048)

        gkm = ar_f32(12288, 1024)
        gkm_cells = arcells(12288, 16384)
        gkf = ar_f32(16384, 1024)
        gkf_cells = arcells(16384, 20480)

        st_cbw = ar_f32(0, 512)
        st_caw = ar_f32(2048, 512)
        st_cfw = ar_f32(4096, DFF)
        st_vec = ar_f32(15360, 640)
        ST_CELLS = arcells(0, 17920)

        def par_loads(e):
            ins = []
            ins.append(e.dma_start(out=st_cbw[0:31, :], in_=conv_b_w))
            ins.append(e.dma_start(out=st_caw[0:3, :], in_=conv_a_w))
            ins.append(e.dma_start(out=st_cfw[0:3, :], in_=conv_ffn_w))
            ins.append(e.dma_start(out=st_vec[0:8, 0:128], in_=norm_mix_g.rearrange("(k p) -> k p", p=128)))
            ins.append(e.dma_start(out=st_vec[0:8, 128:256], in_=norm_ffn_g.rearrange("(k p) -> k p", p=128)))
            ins.append(e.dma_start(out=st_vec[0:4, 256:384], in_=conv_b_b.rearrange("(c p) -> c p", p=128)))
            ins.append(e.dma_start(out=st_vec[0:4, 384:512], in_=ln_b_g.rearrange("(c p) -> c p", p=128)))
            ins.append(e.dma_start(out=st_vec[0:4, 512:640], in_=ln_b_b.rearrange("(c p) -> c p", p=128)))
            ins.append(e.dma_start(out=gfin[:], in_=norm_final_g.partition_broadcast(128)))
            return ins
        P.dma("sp", par_loads, 9, "d_par", writes=ST_CELLS + ["gfin"])

        P.op("pool", lambda e: e.memset(identf[:], 0.0), writes=["identf"])
        P.op("pool", lambda e: e.affine_select(out=identf[:], in_=identf[:], pattern=[[-1, 128]],
                                               compare_op=ALU.not_equal, fill=1.0, base=0, channel_multiplier=1),
             reads=["identf"], writes=["identf"])
        P.op("dve", lambda e: e.tensor_copy(out=identb[:], in_=identf[:]), reads=["identf"], writes=["identb"])
        P.op("dve", lambda e: e.tensor_scalar(out=identh[:], in0=identf[:], scalar1=0.5, scalar2=None, op0=ALU.mult),
             reads=["identf"], writes=["identh"])

        def par_transposes():
            jobs = [
                (0, [(st_cbw[0:31, c * 128:(c + 1) * 128], 31) for c in range(4)], cbw[:, :, :].rearrange("p c t -> p (c t)"), "cbw"),
                (1, [(st_caw[0:3, c * 128:(c + 1) * 128], 3) for c in range(4)], caw[:, :, :].rearrange("p c t -> p (c t)"), "caw"),
                (2, [(st_cfw[0:3, j * 128:(j + 1) * 128], 3) for j in range(NF)], cfw[:, :, :].rearrange("p j t -> p (j t)"), "cfw"),
                (3, [(st_vec[0:8, 0:128], 8)], gmix[:, :], "gmix"),
                (0, [(st_vec[0:8, 128:256], 8)], gffn[:, :], "gffn"),
                (1, [(st_vec[0:4, 256:384], 4)], cbb[:, :], "cbb"),
                (2, [(st_vec[0:4, 384:512], 4)], lng[:, :], "lng"),
                (3, [(st_vec[0:4, 512:640], 4)], lnb[:, :], "lnb"),
            ]
            for bank, items, dst, name in jobs:
                def tr(e, bank=bank, items=items):
                    last = None
                    off = 0
                    for src_ap, r in items:
                        last = e.transpose(out=pb[bank][:, off:off + r], in_=src_ap, identity=identf[0:r, 0:r])
                        off += r
                    return last
                tot = sum(r for _, r in items)
                P.op("pe", tr, reads=ST_CELLS + ["identf"], writes=[PS(bank)])
                P.op("dve", (lambda e, bank=bank, dst=dst, tot=tot: e.tensor_copy(out=dst, in_=pb[bank][:, 0:tot])),
                     reads=[PS(bank)], writes=[name])

        def mk_consts(e):
            e.memset(onesb[:], 1.0 / 512.0)
            e.memset(epsc[:, 0:1], RMS_EPS)
            return e.memset(epsc[:, 1:2], LN_EPS)
        P.op("pool", mk_consts, writes=["onesb", "epsc"])

        def mk_pads(e):
            e.memset(upad[:], 0.0)
            return e.memset(pp[:], 0.0)
        P.op("pool", mk_pads, writes=[("upad", c) for c in range(4)] + [("pp", s) for s in range(2)])
        par_transposes()

        def load_wout():
            P.dma("pool", lambda e: [e.dma_start(out=wout[:], in_=w_out.rearrange("(c p) d -> p c d", p=128))], 1,
                  "d_wout", writes=["wout"])

        def load_wdn_part(q):
            j0 = 6 * q
            j1 = min(NF, j0 + 6)
            P.dma("pool", (lambda e: [e.dma_start(out=wdn[:, j0:j1, :],
                                                  in_=w_down[j0 * 128:j1 * 128, :].rearrange("(j p) d -> p j d", p=128))]),
                  1, "d_wdn%d" % q, writes=[("wdn", q)])
        WDN_CELLS = [("wdn", q) for q in range(4)]

        win_seq = [(ti, e_) for ti in range(len(tiles)) for e_ in A_ORDER]
        win_state = {"next": 0}

        def prefetch_win():
            n = win_state["next"]
            if n >= len(win_seq):
                return
            win_state["next"] = n + 1
            ti_, e_ = win_seq[n]
            s = n % N_WIN_SLOTS
            if ti_ == 0:
                P.dma("pool", (lambda e, s=s, e_=e_: [e.dma_start(
                    out=winr[:, s, :].rearrange("p (k c) -> p k c", k=KC),
                    in_=w_in[:, e_ * 128:(e_ + 1) * 128].rearrange("(k p) c -> p k c", p=128))]),
                    1, "d_winsw%d" % s, writes=[("winr", s)])
                if len(tiles) > 1:
                    P.dma("sp", (lambda e, s=s, e_=e_: [e.dma_start(out=scr_in[e_], in_=winr[:, s, :])]), 1,
                          "d_wst%d" % s, reads=[("winr", s)], writes=[("scr_in", e_)])
            else:
                P.dma("sp", (lambda e, s=s, e_=e_: [e.dma_start(out=winr[:, s, :], in_=scr_in[e_])]), 1, "d_win%d" % s,
                      reads=[("scr_in", e_)], writes=[("winr", s)])

        gu_total = len(tiles) * NF
        gu_state = {"next": 0}

        def prefetch_gu():
            n = gu_state["next"]
            if n >= gu_total:
                return
            gu_state["next"] = n + 1
            j = n % NF
            s = n % N_GU_SLOTS
            if n < NF:
                def ld(e, s=s, j=j):
                    return [e.dma_start(out=gur[:, s, w_, :].rearrange("p (k c) -> p k c", k=KC),
                                        in_=wsrc[:, j * 128:(j + 1) * 128].rearrange("(k p) c -> p k c", p=128))
                            for w_, wsrc in ((0, w_gate), (1, w_up))]
                P.dma("pool", ld, 2, "d_gusw%d" % s, writes=[("gur", s)])
                if len(tiles) > 1:
                    P.dma("sp", (lambda e, s=s, j=j: [e.dma_start(out=scr_gu[j], in_=gur[:, s, :, :])]), 1,
                          "d_gst%d" % s, reads=[("gur", s)], writes=[("scr_gu", j)])
            else:
                P.dma("sp", (lambda e, s=s, j=j: [e.dma_start(out=gur[:, s, :, :], in_=scr_gu[j])]), 1, "d_gu%d" % s,
                      reads=[("scr_gu", j)], writes=[("gur", s)])


        ctr = {"xa": 0, "stat": 0, "tp": 0, "ev": 0, "diag": 0, "hb": 0, "win": 0, "gu": 0, "c1": 0, "tln": 0}

        def nxt(name, mod):
            v = ctr[name]
            ctr[name] = v + 1
            return v % mod

        def rms_chain(src_ap, src_cells, m, junk_ap, junk_cells, eps_col):
            sc = nxt("stat", NSTAT)
            cell = ("stat", sc)
            P.op("act", (lambda e: e.activation(out=junk_ap, in_=src_ap, func=AF.Square, scale=1.0 / 32.0,
                                                accum_out=stat[0:m, sc, 0:1])),
                 reads=src_cells, writes=list(junk_cells) + [cell])
            P.op("act", (lambda e: e.activation(out=stat[0:m, sc, 1:2], in_=stat[0:m, sc, 0:1], func=AF.Sqrt,
                                                bias=epsc[0:m, eps_col:eps_col + 1], scale=1.0)),
                 reads=[cell, "epsc"], writes=[cell])
            P.op("dve", (lambda e: e.reciprocal(out=stat[0:m, sc, 2:3], in_=stat[0:m, sc, 1:2])),
                 reads=[cell], writes=[cell])
            return stat[0:m, sc, 2:3], cell

        def norm_a(src_ap, src_cells, m, buf_ap, buf_cell):
            rs, rcell = rms_chain(src_ap, src_cells, m, buf_ap, [buf_cell], 0)
            P.op("act", (lambda e: e.activation(out=buf_ap, in_=src_ap, func=AF.Copy, scale=rs)),
                 reads=list(src_cells) + [rcell], writes=[buf_cell])

        def norm_b(buf3, buf_cell, m, dstT, dst_cells, col0, gcol, gname):
            tb = nxt("tp", 2)

            def tr(e):
                last = None
                for k in range(KC):
                    last = e.transpose(out=tp[tb][:, k, 0:m], in_=buf3[:, k, :], identity=identb[0:m, 0:m])
                return last
            P.op("pe", tr, reads=[buf_cell, "identb"], writes=[PS(6 + tb)])
            P.op("dve", (lambda e: e.tensor_tensor(out=dstT[:, :, col0:col0 + m], in0=tp[tb][:, :, 0:m],
                                                   in1=gcol[:, 0:KC].unsqueeze(2).broadcast_to([128, KC, m]),
                                                   op=ALU.mult)),
                 reads=[PS(6 + tb), gname], writes=dst_cells)

        def s0_geom(ti):
            t0, T = tiles[ti]
            a0 = max(0, t0 - HALO)
            a1 = min(S, t0 + T + HALO)
            nA = a1 - a0
            return a0, nA, (nA + 127) // 128

        def s0_load(ti, c):
            a0, nA, nch = s0_geom(ti)
            if c >= nch:
                return
            m = min(128, nA - 128 * c)
            s = c % 2
            r0 = a0 + 128 * c
            P.dma("sp", (lambda e, s=s, r0=r0, m=m: [e.dma_start(out=xa[0:m, s, :], in_=x[r0:r0 + m, :])]), 1,
                  "d_xa%d" % s, writes=[("xa", s)])

        def s0_norm(ti, c):
            a0, nA, nch = s0_geom(ti)
            if c >= nch:
                return
            m = min(128, nA - 128 * c)
            s = c % 2
            norm_a(xa[0:m, s, :], [("xa", s)], m, hbs[0:m, c, :], ("hbs", c))

        def stage_S0b(ti, c):
            a0, nA, nch = s0_geom(ti)
            if c >= nch:
                return
            m = min(128, nA - 128 * c)
            norm_b(hbs[0:m, c, :].rearrange("p (k c) -> p k c", k=KC), ("hbs", c), m, hT, [("hT", c)], 128 * c,
                   gmix, "gmix")

        def stage_mixer(ti):
            t0, T = tiles[ti]
            a0 = max(0, t0 - HALO)
            a1 = min(S, t0 + T + HALO)
            nA = a1 - a0
            b0 = max(0, t0 - 1)
            b1 = min(S, t0 + T + 1)
            nB = b1 - b0
            oB = b0 - a0
            ncA = (nA + 127) // 128
            ntc = (nB + 127) // 128

            for tc in range(ntc):
                m = min(128, nB - 128 * tc)
                r0 = b0 + 128 * tc
                P.dma("pool", (lambda e, tc=tc, r0=r0, m=m: [e.dma_start(out=x1[0:m, tc, :], in_=x[r0:r0 + m, :])]), 1,
                      "d_x1_%d" % tc, writes=[("x1", tc)])

            hT_cells = [("hT", c) for c in range(ncA)]
            abank = {"n": 0}
            evslot = {}
            ctslot = {}
            ppslot = {}

            if nA < WIN:
                P.op("pool", (lambda e: e.memset(upad[:, :, 15 + nA:UPW], 0.0)),
                     writes=[("upad", c) for c in range(4)])

            def a_group(e_):
                ws = nxt("win", N_WIN_SLOTS)
                bank = abank["n"] % 4
                abank["n"] += 1

                def mmA(e, ws=ws, bank=bank):
                    last = None
                    for k in range(KC):
                        last = e.matmul(pb[bank][:, 0:nA], lhsT=winr[:, ws, k * 128:(k + 1) * 128], rhs=hT[:, k, 0:nA],
                                        start=(k == 0), stop=(k == KC - 1))
                    return last
                P.op("pe", mmA, reads=[("winr", ws)] + hT_cells, writes=[PS(bank)])
                prefetch_win()
                if ti == 0 and e_ == A_ORDER[-5]:
                    for _ in range(N_GU_SLOTS):
                        prefetch_gu()
                    load_wout()
                    for q in range(4):
                        load_wdn_part(q)
                kind, c = e_ // 4, e_ % 4
                if kind == 4:
                    s = nxt("ev", 3)
                    evslot[("g", c)] = s
                    P.op("act", (lambda e, s=s, bank=bank: e.activation(out=ev[:, s, 0:nA], in_=pb[bank][:, 0:nA],
                                                                         func=AF.Tanh, scale=0.5)),
                         reads=[PS(bank)], writes=[("ev", s)])
                elif kind == 3:
                    s = evslot[("g", c)]
                    P.op("dve", (lambda e, s=s, bank=bank, c=c: e.scalar_tensor_tensor(
                        out=upad[:, c, 15:15 + nA], in0=ev[:, s, 0:nA], scalar=1.0, in1=pb[bank][:, 0:nA],
                        op0=ALU.add, op1=ALU.mult)),
                        reads=[("ev", s), PS(bank)], writes=[("upad", c)])
                elif kind == 0:
                    s = nxt("ev", 3)
                    evslot[("h", c)] = s
                    P.op("act", (lambda e, s=s, bank=bank: e.activation(out=ev[:, s, 0:nA], in_=pb[bank][:, 0:nA],
                                                                         func=AF.Copy)),
                         reads=[PS(bank)], writes=[("ev", s)])
                elif kind == 2:
                    s = evslot[("h", c)]
                    ps_ = c % 2
                    cs = c
                    ppslot[c] = ps_
                    ctslot[c] = cs
                    P.op("dve", (lambda e, s=s, bank=bank, ps_=ps_: e.tensor_tensor(
                        out=pp[:, ps_, 1:1 + nA], in0=pb[bank][:, 0:nA], in1=ev[:, s, 0:nA], op=ALU.mult)),
                        reads=[("ev", s), PS(bank)], writes=[("pp", ps_)])
                    if nA < WIN:
                        P.op("pool", (lambda e, ps_=ps_: e.memset(pp[:, ps_, 1 + nA:PPW], 0.0)), writes=[("pp", ps_)])
                    P.op("act", (lambda e, ps_=ps_, cs=cs, c=c: e.activation(
                        out=ct[:, cs, 0:nB], in_=pp[:, ps_, 1 + oB:1 + oB + nB], func=AF.Copy, scale=caw[:, c, 1:2])),
                        reads=[("pp", ps_), "caw"], writes=[("ct", cs)])
                    P.op("dve", (lambda e, ps_=ps_, cs=cs, c=c: e.scalar_tensor_tensor(
                        out=ct[:, cs, 0:nB], in0=pp[:, ps_, oB:oB + nB], scalar=caw[:, c, 0:1], in1=ct[:, cs, 0:nB],
                        op0=ALU.mult, op1=ALU.add)), reads=[("pp", ps_), ("ct", cs), "caw"], writes=[("ct", cs)])
                    P.op("dve", (lambda e, ps_=ps_, cs=cs, c=c: e.scalar_tensor_tensor(
                        out=ct[:, cs, 0:nB], in0=pp[:, ps_, 2 + oB:2 + oB + nB], scalar=caw[:, c, 2:3],
                        in1=ct[:, cs, 0:nB], op0=ALU.mult, op1=ALU.add)),
                        reads=[("pp", ps_), ("ct", cs), "caw"], writes=[("ct", cs)])
                elif kind == 1:
                    cs = ctslot[c]
                    P.op("dve", (lambda e, cs=cs, bank=bank, c=c: e.tensor_tensor(
                        out=yT[:, c, 0:nB], in0=pb[bank][:, oB:oB + nB], in1=ct[:, cs, 0:nB], op=ALU.mult)),
                        reads=[("ct", cs), PS(bank)], writes=[("yT", c)])

            diag_q = []

            def gen_diag(idx):
                c, kb = idx // 4, idx % 4
                k0 = 8 * kb
                nt = min(8, 31 - k0)
                ds = nxt("diag", N_DIAG_BLK)
                P.op(DIAG_ENGINE, (lambda e, ds=ds, c=c, k0=k0, nt=nt: e.tensor_tensor(
                    out=diag[:, ds, 0:nt, :], in0=identh[:].unsqueeze(1).broadcast_to([128, nt, 128]),
                    in1=cbw[:, c, k0:k0 + nt].unsqueeze(2).broadcast_to([128, nt, 128]), op=ALU.mult)),
                    reads=["identh", "cbw"], writes=[("diag", ds)])
                diag_q.append(ds)

            diag_state = {"gen": 0}

            def ensure_diag(upto):
                while diag_state["gen"] <= upto and diag_state["gen"] < 16:
                    gen_diag(diag_state["gen"])
                    diag_state["gen"] += 1

            def conv_chunk(c):
                bank = 4 + (c % 2)
                for kb in range(4):
                    k0 = 8 * kb
                    nt = min(8, 31 - k0)
                    idx = 4 * c + kb
                    ensure_diag(idx)
                    ds = diag_q[idx]

                    def mmC(e, ds=ds, c=c, k0=k0, nt=nt, bank=bank):
                        last = None
                        for t in range(nt):
                            k = k0 + t
                            last = e.matmul(pb[bank][:, 0:nB], lhsT=diag[:, ds, t, :],
                                            rhs=upad[:, c, oB + k:oB + k + nB], start=(k == 0), stop=(k == 30))
                        return last
                    P.op("pe", mmC, reads=[("diag", ds), ("upad", c)], writes=[PS(bank)])
                    ensure_diag(min(15, idx + N_DIAG_BLK - 1))
                P.op("act", (lambda e, c=c, bank=bank: e.activation(out=ucbv(c)[:, 0:nB], in_=pb[bank][:, 0:nB],
                                                                     func=AF.Identity, bias=cbb[:, c:c + 1], scale=1.0)),
                     reads=[PS(bank), "cbb"], writes=ucb_cells(c))
                P.op("act", (lambda e, c=c, bank=bank: e.activation(out=usqv(c)[:, 0:nB], in_=pb[bank][:, 0:nB],
                                                                     func=AF.Square, bias=cbb[:, c:c + 1], scale=1.0)),
                     reads=[PS(bank), "cbb"], writes=usq_cells(c))
                P.op("act", (lambda e, c=c, bank=bank: e.activation(out=ucv(c)[:, 0:nB], in_=pb[bank][:, 0:nB],
                                                                     func=AF.Identity, bias=cbb[:, c:c + 1], scale=1.0)),
                     reads=[PS(bank), "cbb"], writes=uc_cells(c))

            def stat_chunk(c):
                P.op("pe", (lambda e, c=c: e.matmul(stat_ps[0][:, 0:nB], lhsT=onesb[:], rhs=ucbv(c)[:, 0:nB],
                                                     start=(c == 0), stop=(c == 3))),
                     reads=["onesb"] + ucb_cells(c), writes=[PS(6)])
                P.op("pe", (lambda e, c=c: e.matmul(stat_ps[1][:, 0:nB], lhsT=onesb[:], rhs=usqv(c)[:, 0:nB],
                                                     start=(c == 0), stop=(c == 3))),
                     reads=["onesb"] + usq_cells(c), writes=[PS(7)])

            ensure_diag(N_DIAG_BLK - 2)
            for e_ in A_ORDER[:8]:
                a_group(e_)
            conv_chunk(0)
            conv_chunk(1)
            stat_chunk(0)
            conv_chunk(2)
            stat_chunk(1)
            conv_chunk(3)
            stat_chunk(2)
            stat_chunk(3)

            L = []
            L.append(lambda: P.op("act", (lambda e: e.activation(out=mean_sb[:, 0:nB], in_=stat_ps[0][:, 0:nB],
                                                                 func=AF.Copy)), reads=[PS(6)], writes=mean_cells))
            L.append(lambda: P.op("act", (lambda e: e.activation(out=var_sb[:, 0:nB], in_=stat_ps[0][:, 0:nB],
                                                                 func=AF.Square)), reads=[PS(6)], writes=var_cells))
            L.append(lambda: P.op("dve", (lambda e: e.tensor_tensor(out=var_sb[:, 0:nB], in0=stat_ps[1][:, 0:nB],
                                                                    in1=var_sb[:, 0:nB], op=ALU.subtract)),
                                  reads=[PS(7)] + var_cells, writes=var_cells))
            L.append(lambda: P.op("act", (lambda e: e.activation(out=var_sb[:, 0:nB], in_=var_sb[:, 0:nB], func=AF.Sqrt,
                                                                 bias=epsc[:, 1:2], scale=1.0)),
                                  reads=var_cells + ["epsc"], writes=var_cells))
            L.append(lambda: P.op("dve", (lambda e: e.reciprocal(out=var_sb[:, 0:nB], in_=var_sb[:, 0:nB])),
                                  reads=var_cells, writes=var_cells))
            LN_mul, LN_sub, LN_silu = [], [], []
            for c in range(4):
                eng_c = "dve" if c % 2 == 0 else LN_ENGINE
                LN_sub.append(lambda c=c, eng_c=eng_c: P.op(eng_c, (lambda e: e.tensor_tensor(
                    out=ucv(c)[:, 0:nB], in0=ucv(c)[:, 0:nB], in1=mean_sb[:, 0:nB], op=ALU.subtract)),
                    reads=uc_cells(c) + mean_cells, writes=uc_cells(c)))
                LN_mul.append(lambda c=c, eng_c=eng_c: P.op(eng_c, (lambda e: e.tensor_tensor(
                    out=ucv(c)[:, 0:nB], in0=ucv(c)[:, 0:nB], in1=var_sb[:, 0:nB], op=ALU.mult)),
                    reads=uc_cells(c) + var_cells, writes=uc_cells(c)))
                LN_silu.append(lambda c=c: P.op("act", (lambda e: e.activation(
                    out=yT[:, 4 + c, 0:nB], in_=ucv(c)[:, 0:nB], func=AF.Silu, scale=lng[:, c:c + 1],
                    bias=lnb[:, c:c + 1])), reads=uc_cells(c) + ["lng", "lnb"], writes=[("yT", 4 + c)]))
            sched = {
                0: [L[0], L[1]],
                1: [L[2], LN_sub[0], LN_sub[1]],
                2: [L[3], LN_sub[2], LN_sub[3]],
                4: [L[4]],
                5: [LN_mul[0], LN_mul[1]],
                6: [LN_mul[2], LN_mul[3]],
                8: [LN_silu[0], LN_silu[1]],
                9: [LN_silu[2], LN_silu[3]],
            }
            for gi, e_ in enumerate(A_ORDER[8:]):
                a_group(e_)
                for th in sched.get(gi, []):
                    th()

            yT_cells = [("yT", c) for c in range(8)]
            obank = {"n": 0}
            pend_b = None
            for tc in range(ntc):
                m = min(128, nB - 128 * tc)
                for dh in range(2):
                    bank = 4 + (obank["n"] % 2)
                    obank["n"] += 1

                    def mmO(e, tc=tc, m=m, dh=dh, bank=bank):
                        last = None
                        for cc in range(KC):
                            last = e.matmul(pb[bank][0:m, :], lhsT=yT[:, cc, 128 * tc:128 * tc + m],
                                            rhs=wout[:, cc, dh * 512:(dh + 1) * 512], start=(cc == 0), stop=(cc == KC - 1))
                        return last
                    P.op("pe", mmO, reads=yT_cells + ["wout"], writes=[PS(bank)])
                    P.op("dve", (lambda e, tc=tc, m=m, dh=dh, bank=bank: e.tensor_tensor(
                        out=x1[0:m, tc, dh * 512:(dh + 1) * 512], in0=pb[bank][0:m, :],
                        in1=x1[0:m, tc, dh * 512:(dh + 1) * 512], op=ALU.add)),
                        reads=[PS(bank), ("x1", tc)], writes=[("x1", tc)])
                hs = nxt("hb", 3)
                norm_a(x1[0:m, tc, :], [("x1", tc)], m, hbf[0:m, hs, :], ("hbf", hs))
                if pend_b is not None:
                    pend_b()

                def pb_(tc=tc, m=m, hs=hs):
                    norm_b(hbf[0:m, hs, :].rearrange("p (k c) -> p k c", k=KC), ("hbf", hs), m, h2T, [("h2T", tc)],
                           128 * tc, gffn, "gffn")
                pend_b = pb_
            return nB, ntc, b0, pend_b

        def stage_ffn(ti, nB, ntc, b0, hoist, pend_b):
            t0, T = tiles[ti]
            h2T_cells = [("h2T", tc) for tc in range(ntc)]
            pending_mul = None
            tails = []
            evq = []
            if ntc <= 1:
                pend_b()
            for j in range(NF):
                gs = nxt("gu", N_GU_SLOTS)
                if j < 18:
                    gb = (0, 1, 2)[j % 3]
                    vb = (4, 5, 3)[j % 3]
                else:
                    gb = (0, 1, 6)[j % 3]
                    vb = (4, 5, 7)[j % 3]

                def mk_mm(w_, bank, c0, c1, gs=gs):
                    def mm(e):
                        last = None
                        for k in range(KC):
                            last = e.matmul(pbv(bank)[:, c0:c1], lhsT=gur[:, gs, w_, k * 128:(k + 1) * 128],
                                            rhs=h2T[:, k, c0:c1], start=(k == 0), stop=(k == KC - 1))
                        return last
                    return mm
                if j < 2 and ntc > 1:
                    c_head = 128 * max(1, ntc - 2)
                    pieces = [(0, c_head)]
                    if ntc > 2:
                        pieces.append((c_head, 128 * (ntc - 1)))
                    pieces.append((128 * (ntc - 1), nB))
                    hd = h2T_cells[:max(1, ntc - 2)]
                    P.op("pe", mk_mm(0, gb, 0, c_head), reads=[("gur", gs)] + hd, writes=[PS(gb)])
                    P.op("pe", mk_mm(1, vb, 0, c_head), reads=[("gur", gs)] + hd, writes=[PS(vb)])
                    tails.append((gs, gb, vb))
                    if j == 1:
                        pend_b()
                        for pi_, (c0_, c1_) in enumerate(pieces[1:]):
                            need = h2T_cells[:(c1_ + 127) // 128]
                            for (gs_, gb_, vb_) in tails:
                                P.op("pe", mk_mm(0, gb_, c0_, c1_, gs_), reads=[("gur", gs_)] + need, writes=[PS(gb_)])
                                P.op("pe", mk_mm(1, vb_, c0_, c1_, gs_), reads=[("gur", gs_)] + need, writes=[PS(vb_)])
                else:
                    P.op("pe", mk_mm(0, gb, 0, nB), reads=[("gur", gs)] + h2T_cells, writes=[PS(gb)])
                    P.op("pe", mk_mm(1, vb, 0, nB), reads=[("gur", gs)] + h2T_cells, writes=[PS(vb)])
                if j < 2 and ntc > 1:
                    if j == 1:
                        prefetch_gu()
                        prefetch_gu()
                else:
                    prefetch_gu()
                def evac(j=j, gb=gb, vb=vb):
                    nonlocal pending_mul
                    cs = nxt("c1", 2)
                    cc_ = c1_cells(cs)
                    P.op("act", (lambda e, cs=cs, gb=gb, j=j: e.activation(out=c1v(cs)[:, 0:nB], in_=pbv(gb)[:, 0:nB],
                                                                            func=AF.Copy, scale=cfw[:, j, 1:2])),
                         reads=[PS(gb), "cfw"], writes=cc_)
                    P.op("dve", (lambda e, cs=cs, gb=gb, j=j: e.scalar_tensor_tensor(
                        out=c1v(cs)[:, 1:nB], in0=pbv(gb)[:, 0:nB - 1], scalar=cfw[:, j, 0:1], in1=c1v(cs)[:, 1:nB],
                        op0=ALU.mult, op1=ALU.add)), reads=[PS(gb), "cfw"] + cc_, writes=cc_)
                    if pending_mul is not None:
                        pending_mul()
                    P.op("dve", (lambda e, cs=cs, gb=gb, j=j: e.scalar_tensor_tensor(
                        out=c1v(cs)[:, 0:nB - 1], in0=pbv(gb)[:, 1:nB], scalar=cfw[:, j, 2:3], in1=c1v(cs)[:, 0:nB - 1],
                        op0=ALU.mult, op1=ALU.add)), reads=[PS(gb), "cfw"] + cc_, writes=cc_)
                    P.op("act", (lambda e, cs=cs: e.activation(out=c1v(cs)[:, 0:nB], in_=c1v(cs)[:, 0:nB], func=AF.Silu)),
                         reads=cc_, writes=cc_)

                    def mul(cs=cs, vb=vb, j=j, cc_=cc_):
                        P.op("dve", (lambda e: e.tensor_tensor(out=aT(j)[:, 0:nB], in0=pbv(vb)[:, 0:nB],
                                                               in1=c1v(cs)[:, 0:nB], op=ALU.mult)),
                             reads=[PS(vb)] + cc_, writes=aT_cells(j))
                    pending_mul = mul
                if j < 2 and ntc > 1:
                    evq.append(evac)
                    if j == 1:
                        for ev_ in evq:
                            ev_()
                else:
                    evac()
            pending_mul()

            aT_all = [c for j in range(NF) for c in aT_cells(j)]
            dbank = {"n": 0}
            if hoist is not None:
                s0_load(hoist, 0)
                s0_load(hoist, 1)
            for tc in range(ntc):
                m = min(128, nB - 128 * tc)
                for dh in range(2):
                    bank = 2 + (dbank["n"] % 2)
                    dbank["n"] += 1

                    def mmD(e, tc=tc, m=m, dh=dh, bank=bank):
                        last = None
                        for j in range(NF):
                            last = e.matmul(pb[bank][0:m, :], lhsT=aT(j)[:, 128 * tc:128 * tc + m],
                                            rhs=wdn[:, j, dh * 512:(dh + 1) * 512], start=(j == 0), stop=(j == NF - 1))
                        return last
                    if dbank["n"] == 1:
                        def mmD1(e, tc=tc, m=m, dh=dh, bank=bank):
                            last = None
                            for j in range(NF - 4):
                                last = e.matmul(pb[bank][0:m, :], lhsT=aT(j)[:, 128 * tc:128 * tc + m],
                                                rhs=wdn[:, j, dh * 512:(dh + 1) * 512], start=(j == 0), stop=False)
                            return last

                        def mmD2(e, tc=tc, m=m, dh=dh, bank=bank):
                            last = None
                            for j in range(NF - 4, NF):
                                last = e.matmul(pb[bank][0:m, :], lhsT=aT(j)[:, 128 * tc:128 * tc + m],
                                                rhs=wdn[:, j, dh * 512:(dh + 1) * 512], start=False, stop=(j == NF - 1))
                            return last
                        head_cells = [c for j in range(NF - 4) for c in aT_cells(j)]
                        P.op("pe", mmD1, reads=head_cells + WDN_CELLS, writes=[PS(bank)])
                        P.op("pe", mmD2, reads=aT_all + WDN_CELLS, writes=[PS(bank)])
                    else:
                        P.op("pe", mmD, reads=aT_all + WDN_CELLS, writes=[PS(bank)])
                    if hoist is not None:
                        g_ = dbank["n"] - 1
                        if g_ == 0:
                            s0_norm(hoist, 0)
                            s0_load(hoist, 2)
                        elif g_ == 1:
                            s0_norm(hoist, 1)
                            s0_load(hoist, 3)
                        elif g_ == 2:
                            stage_S0b(hoist, 0)
                            s0_norm(hoist, 2)
                        elif g_ == 3:
                            stage_S0b(hoist, 1)
                            s0_norm(hoist, 3)
                        elif g_ == 4:
                            stage_S0b(hoist, 2)
                        elif g_ == 5:
                            stage_S0b(hoist, 3)
                    P.op("dve", (lambda e, tc=tc, m=m, dh=dh, bank=bank: e.tensor_tensor(
                        out=x1[0:m, tc, dh * 512:(dh + 1) * 512], in0=pb[bank][0:m, :],
                        in1=x1[0:m, tc, dh * 512:(dh + 1) * 512], op=ALU.add)),
                        reads=[PS(bank), ("x1", tc)], writes=[("x1", tc)])
                hs = nxt("hb", 3)
                rs, rcell = rms_chain(x1[0:m, tc, :], [("x1", tc)], m, hbf[0:m, hs, :], [("hbf", hs)], 0)
                P.op("dve", (lambda e, tc=tc, m=m, rs=rs: e.scalar_tensor_tensor(
                    out=x1[0:m, tc, :], in0=x1[0:m, tc, :], scalar=rs, in1=gfin[0:m, :], op0=ALU.mult, op1=ALU.mult)),
                    reads=[("x1", tc), rcell, "gfin"], writes=[("x1", tc)])
                r_lo = max(0, (t0 - b0) - 128 * tc)
                r_hi = min(m, (t0 + T - b0) - 128 * tc)
                if r_lo < r_hi:
                    g0 = b0 + 128 * tc
                    tok = P.dma("pool", (lambda e, tc=tc, r_lo=r_lo, r_hi=r_hi, g0=g0: [e.dma_start(
                        out=y[g0 + r_lo:g0 + r_hi, :], in_=x1[r_lo:r_hi, tc, :])]), 1, "d_x1_%d" % tc,
                        reads=[("x1", tc)])
                    P.final_tokens.append(tok)

        s0_load(0, 0)
        s0_load(0, 1)
        s0_norm(0, 0)
        s0_load(0, 2)
        s0_norm(0, 1)
        s0_load(0, 3)
        for _ in range(N_WIN_SLOTS):
            prefetch_win()
        stage_S0b(0, 0)
        stage_S0b(0, 1)
        s0_norm(0, 2)
        stage_S0b(0, 2)
        s0_norm(0, 3)
        stage_S0b(0, 3)
        for ti in range(len(tiles)):
            nB, ntc, b0, pend_b = stage_mixer(ti)
            hoist = ti + 1 if ti + 1 < len(tiles) else None
            stage_ffn(ti, nB, ntc, b0, hoist, pend_b)

        P.emit(nc, es)
    return nc


_PARAM_KEYS = ["norm_mix_g", "w_in", "conv_a_w", "conv_b_w", "conv_b_b", "ln_b_g", "ln_b_b", "w_out",
               "norm_ffn_g", "w_gate", "w_up", "conv_ffn_w", "w_down", "norm_final_g"]


def _prep_params(inputs):
    out = {}
    for k in _PARAM_KEYS:
        a = np.asarray(inputs[k], dtype=np.float32)
        if k != "norm_final_g":
            a = a[0]
        out[k] = np.ascontiguousarray(a)
    return out


def kernel(**inputs):
    x = np.asarray(inputs["x"], dtype=np.float32)
    B, S, _ = x.shape
    params = _prep_params(inputs)
    nc = build(S)
    in_maps = []
    for b in range(B):
        m = dict(params)
        m["x"] = np.ascontiguousarray(x[b])
        in_maps.append(m)
    res = run_bass_kernel_spmd(nc, in_maps, core_ids=list(range(B)))
    return np.stack([np.asarray(r["y"], dtype=np.float32) for r in res.results], axis=0)
```

```python
import numpy as np
from contextlib import ExitStack

import concourse.bass as bass
import concourse.mybir as mybir
from concourse.bass_utils import run_bass_kernel_spmd

F32 = mybir.dt.float32
BF16 = mybir.dt.bfloat16
AF = mybir.ActivationFunctionType
ALU = mybir.AluOpType

D = 1024
DA = 512
DB = 512
DIN = 2560
DFF = 2816
NF = DFF // 128
KC = D // 128
NE = DIN // 128
RMS_EPS = 1e-6
LN_EPS = 1e-5
HALO = 16
WIN = 512

A_ORDER = [16, 12, 17, 13, 18, 14, 19, 15, 0, 8, 1, 9, 2, 10, 3, 11, 4, 5, 6, 7]

N_WIN_SLOTS = 5
N_GU_SLOTS = 3
N_DIAG_BLK = 3
DIAG_ENGINE = "dve"
LN_ENGINE = "pool"


def plan_tiles(S):
    tiles = []
    t0 = 0
    while t0 < S:
        if t0 == 0:
            T = min(S, WIN - HALO)
        else:
            T = min(WIN - 2 * HALO, S - t0)
            if S - t0 <= WIN - HALO:
                T = S - t0
        tiles.append((t0, T))
        t0 += T
    return tiles


class Prog:
    COMPUTE = ("act", "dve", "pool", "pe")

    def __init__(self):
        self.streams = {e: [] for e in ("sp", "act", "dve", "pool", "pe")}
        self.ccount = {e: 0 for e in self.COMPUTE}
        self.dcount = {}
        self.state = {}
        self.waited = {e: {} for e in self.streams}
        self.semnames = set("c_" + e for e in self.COMPUTE)
        self.final_tokens = []

    def _deps(self, eng, reads, writes, is_dma):
        need = {}

        def add(tok, raw):
            sem, val, src = tok
            if src == eng and not is_dma:
                if eng == "pe":
                    return
                if not raw:
                    return
            if need.get(sem, 0) < val:
                need[sem] = val

        for c in reads:
            st = self.state.get(c)
            if st is not None and st[0] is not None:
                add(st[0], True)
        for c in writes:
            st = self.state.get(c)
            if st is not None:
                if st[0] is not None:
                    add(st[0], False)
                for t in st[1].values():
                    add(t, False)
        waits = []
        wd = self.waited[eng]
        for sem, val in need.items():
            if wd.get(sem, 0) < val:
                wd[sem] = val
                waits.append((sem, val))
        return waits

    def _commit(self, tok, reads, writes):
        for c in reads:
            st = self.state.get(c)
            if st is None:
                st = [None, {}]
                self.state[c] = st
            st[1][(tok[0], tok[2])] = tok
        for c in writes:
            self.state[c] = [tok, {}]

    def op(self, eng, fn, reads=(), writes=()):
        reads = tuple(reads)
        writes = tuple(writes)
        waits = self._deps(eng, reads, writes, False)
        self.ccount[eng] += 1
        tok = ("c_" + eng, self.ccount[eng], eng)
        self.streams[eng].append((waits, fn, "c_" + eng, 1))
        self._commit(tok, reads, writes)
        return tok

    def dma(self, queue, fn, n, sem, reads=(), writes=()):
        reads = tuple(reads)
        writes = tuple(writes)
        waits = self._deps(queue, reads, writes, True)
        self.semnames.add(sem)
        self.dcount[sem] = self.dcount.get(sem, 0) + n
        tok = (sem, 16 * self.dcount[sem], queue)
        self.streams[queue].append((waits, fn, sem, 16))
        self._commit(tok, reads, writes)
        return tok

    def emit(self, nc, es):
        sems = {name: es.enter_context(nc.semaphore(name)) for name in sorted(self.semnames)}
        block = es.enter_context(nc.Block())
        streams = self.streams
        finals = self.final_tokens

        awaited = set()
        for name in streams:
            for waits, fn, sem, inc in streams[name]:
                for s, v in waits:
                    awaited.add((s, v))
        remap = {}
        for name in self.COMPUTE:
            sem = "c_" + name
            new = 0
            idx = 0
            m = {}
            for waits, fn, s_, inc in streams[name]:
                if inc != 1:
                    continue
                idx += 1
                if (sem, idx) in awaited:
                    new += 1
                    m[idx] = new
            remap[sem] = m

        def run(eng, name):
            idx = 0
            for waits, fn, sem, inc in streams[name]:
                for s, v in waits:
                    if s in remap:
                        v = remap[s][v]
                    eng.wait_ge(sems[s], v)
                res = fn(eng)
                if inc == 16:
                    for ins in res:
                        ins.then_inc(sems[sem], 16)
                else:
                    idx += 1
                    if idx in remap[sem]:
                        last = res[-1] if isinstance(res, (list, tuple)) else res
                        last.then_inc(sems[sem], 1)
            if name == "sp":
                done = {}
                for s, v, _ in finals:
                    done[s] = max(done.get(s, 0), v)
                for s, v in done.items():
                    eng.wait_ge(sems[s], v)

        @block.sync
        def _(e):
            run(e, "sp")

        @block.scalar
        def _(e):
            run(e, "act")

        @block.vector
        def _(e):
            run(e, "dve")

        @block.gpsimd
        def _(e):
            run(e, "pool")

        @block.tensor
        def _(e):
            run(e, "pe")


def arcells(lo, hi):
    return [("ar", i) for i in range(lo // 1024, (hi - 1) // 1024 + 1)]


def build(S):
    nc = bass.Bass("TRN2", target_bir_lowering=False)
    dt = nc.dram_tensor
    x = dt("x", [S, D], F32, kind="ExternalInput").ap()
    norm_mix_g = dt("norm_mix_g", [D], F32, kind="ExternalInput").ap()
    w_in = dt("w_in", [D, DIN], F32, kind="ExternalInput").ap()
    conv_a_w = dt("conv_a_w", [3, DA], F32, kind="ExternalInput").ap()
    conv_b_w = dt("conv_b_w", [31, DB], F32, kind="ExternalInput").ap()
    conv_b_b = dt("conv_b_b", [DB], F32, kind="ExternalInput").ap()
    ln_b_g = dt("ln_b_g", [DB], F32, kind="ExternalInput").ap()
    ln_b_b = dt("ln_b_b", [DB], F32, kind="ExternalInput").ap()
    w_out = dt("w_out", [D, D], F32, kind="ExternalInput").ap()
    norm_ffn_g = dt("norm_ffn_g", [D], F32, kind="ExternalInput").ap()
    w_gate = dt("w_gate", [D, DFF], F32, kind="ExternalInput").ap()
    w_up = dt("w_up", [D, DFF], F32, kind="ExternalInput").ap()
    conv_ffn_w = dt("conv_ffn_w", [3, DFF], F32, kind="ExternalInput").ap()
    w_down = dt("w_down", [DFF, D], F32, kind="ExternalInput").ap()
    norm_final_g = dt("norm_final_g", [D], F32, kind="ExternalInput").ap()
    y = dt("y", [S, D], F32, kind="ExternalOutput").ap()
    scr_in = dt("scr_in", [NE, 128, 1024], BF16, kind="Internal").ap()
    scr_gu = dt("scr_gu", [NF, 128, 2, 1024], BF16, kind="Internal").ap()

    tiles = plan_tiles(S)
    P = Prog()

    with ExitStack() as es:
        def sb(name, shape, dtype):
            return es.enter_context(nc.sbuf_tensor(name, shape, dtype))

        wout = sb("wout", [128, KC, D], BF16)
        wdn = sb("wdn", [128, NF, D], BF16)
        winr = sb("winr", [128, N_WIN_SLOTS, 1024], BF16)
        gur = sb("gur", [128, N_GU_SLOTS, 2, 1024], BF16)
        diag = sb("diag", [128, N_DIAG_BLK, 8, 128], BF16)
        gfin = sb("gfin", [128, D], F32)
        identf = sb("identf", [128, 128], F32)
        identh = sb("identh", [128, 128], F32)
        identb = sb("identb", [128, 128], BF16)
        onesb = sb("onesb", [128, 128], BF16)
        gmix = sb("gmix", [128, KC], F32)
        gffn = sb("gffn", [128, KC], F32)
        caw = sb("caw", [128, 4, 3], F32)
        cbw = sb("cbw", [128, 4, 31], F32)
        cbb = sb("cbb", [128, 4], F32)
        lng = sb("lng", [128, 4], F32)
        lnb = sb("lnb", [128, 4], F32)
        cfw = sb("cfw", [128, NF, 3], F32)
        epsc = sb("epsc", [128, 2], F32)
        xa = sb("xa", [128, 2, D], F32)
        hbf = sb("hbf", [128, 3, D], BF16)
        hbs = sb("hbs", [128, 4, D], BF16)
        hT = sb("hT", [128, KC, WIN], BF16)
        h2T = sb("h2T", [128, KC, WIN], BF16)
        yT = sb("yT", [128, KC, WIN], BF16)
        x1 = sb("x1", [128, 4, D], F32)
        NSTAT = 16
        stat = sb("stat", [128, NSTAT, 4], F32)
        ev = sb("ev", [128, 3, WIN], F32)
        UPW = 544
        upad = sb("upad", [128, 4, UPW], BF16)
        PPW = 516
        pp = sb("pp", [128, 2, PPW], F32)
        ct = sb("ct", [128, 4, WIN], F32)
        ARENA_E = 13312
        arena = sb("arena", [128, ARENA_E], BF16)

        pb = [es.enter_context(nc.psum_tensor("pb%d" % i, [128, 512], F32)) for i in range(6)]
        tp = [es.enter_context(nc.psum_tensor("tp%d" % i, [128, KC, 128], BF16)) for i in range(2)]

        def PS(b):
            return ("ps", b)

        stat_ps = [tp[b][:, :, :].rearrange("p k c -> p (k c)").bitcast(F32) for b in range(2)]

        def pbv(b):
            return pb[b] if b < 6 else stat_ps[b - 6]

        def ar_bf(lo_b, n):
            return arena[:, lo_b // 2: lo_b // 2 + n]

        def ar_f32(lo_b, n):
            return arena[:, lo_b // 2: lo_b // 2 + 2 * n].bitcast(F32)

        def aT(j):
            return ar_bf(j * 1024, 512)

        def aT_cells(j):
            return arcells(j * 1024, (j + 1) * 1024)

        def c1v(s):
            return ar_f32(22528 + s * 2048, 512)

        def c1_cells(s):
            return arcells(22528 + s * 2048, 22528 + (s + 1) * 2048)

        uc_all = ar_f32(0, 2048)

        def ucv(c):
            return uc_all[:, c * 512:(c + 1) * 512]

        def uc_cells(c):
            return arcells(c * 2048, (c + 1) * 2048)

        def ucbv(c):
            return ar_bf(8192 + c * 1024, 512)

        def ucb_cells(c):
            return arcells(8192 + c * 1024, 8192 + (c + 1) * 1024)

        def usqv(c):
            return ar_bf(12288 + c * 1024, 512)

        def usq_cells(c):
            return arcells(12288 + c * 1024, 12288 + (c + 1) * 1024)

        mean_sb = ar_f32(16384, 512)
        mean_cells = arcells(16384, 18432)
        var_sb = ar_f32(18432, 512)
        var_cells = arcells(18432, 20480)
        mr_sb = ar_f32(20480, 512)
        mr_cells = arcells(20480, 22528)

        def tlnv(s):
            return c1v(s)

        def tln_cells(s):
            return c1_cells(s)

        def stgv(s):
            return ar_f32(s * 4096, 1024)

        def stg_cells(s):
            return arcells(s * 4096, (s + 1) * 4096)

        def obv(s):
            return ar_bf(8192 + s * 2048, 1024)

        def ob_cells(s):
            return arcells(8192 + s * 2048, 8192 + (s + 1) * 2048)

        gkm = ar_f32(12288, 1024)
        gkm_cells = arcells(12288, 16384)
        gkf = ar_f32(16384, 1024)
        gkf_cells = arcells(16384, 20480)

        st_cbw = ar_f32(0, 512)
        st_caw = ar_f32(2048, 512)
        st_cfw = ar_f32(4096, DFF)
        st_vec = ar_f32(15360, 640)
        ST_CELLS = arcells(0, 17920)

        def par_loads(e):
            ins = []
            ins.append(e.dma_start(out=st_cbw[0:31, :], in_=conv_b_w))
            ins.append(e.dma_start(out=st_caw[0:3, :], in_=conv_a_w))
            ins.append(e.dma_start(out=st_cfw[0:3, :], in_=conv_ffn_w))
            ins.append(e.dma_start(out=st_vec[0:8, 0:128], in_=norm_mix_g.rearrange("(k p) -> k p", p=128)))
            ins.append(e.dma_start(out=st_vec[0:8, 128:256], in_=norm_ffn_g.rearrange("(k p) -> k p", p=128)))
            ins.append(e.dma_start(out=st_vec[0:4, 256:384], in_=conv_b_b.rearrange("(c p) -> c p", p=128)))
            ins.append(e.dma_start(out=st_vec[0:4, 384:512], in_=ln_b_g.rearrange("(c p) -> c p", p=128)))
            ins.append(e.dma_start(out=st_vec[0:4, 512:640], in_=ln_b_b.rearrange("(c p) -> c p", p=128)))
            ins.append(e.dma_start(out=gfin[:], in_=norm_final_g.partition_broadcast(128)))
            return ins
        P.dma("sp", par_loads, 9, "d_par", writes=ST_CELLS + ["gfin"])

        P.op("pool", lambda e: e.memset(identf[:], 0.0), writes=["identf"])
        P.op("pool", lambda e: e.affine_select(out=identf[:], in_=identf[:], pattern=[[-1, 128]],
                                               compare_op=ALU.not_equal, fill=1.0, base=0, channel_multiplier=1),
             reads=["identf"], writes=["identf"])
        P.op("dve", lambda e: e.tensor_copy(out=identb[:], in_=identf[:]), reads=["identf"], writes=["identb"])
        P.op("dve", lambda e: e.tensor_scalar(out=identh[:], in0=identf[:], scalar1=0.5, scalar2=None, op0=ALU.mult),
             reads=["identf"], writes=["identh"])

        def par_transposes():
            jobs = [
                (0, [(st_cbw[0:31, c * 128:(c + 1) * 128], 31) for c in range(4)], cbw[:, :, :].rearrange("p c t -> p (c t)"), "cbw"),
                (1, [(st_caw[0:3, c * 128:(c + 1) * 128], 3) for c in range(4)], caw[:, :, :].rearrange("p c t -> p (c t)"), "caw"),
                (2, [(st_cfw[0:3, j * 128:(j + 1) * 128], 3) for j in range(NF)], cfw[:, :, :].rearrange("p j t -> p (j t)"), "cfw"),
                (3, [(st_vec[0:8, 0:128], 8)], gmix[:, :], "gmix"),
                (0, [(st_vec[0:8, 128:256], 8)], gffn[:, :], "gffn"),
                (1, [(st_vec[0:4, 256:384], 4)], cbb[:, :], "cbb"),
                (2, [(st_vec[0:4, 384:512], 4)], lng[:, :], "lng"),
                (3, [(st_vec[0:4, 512:640], 4)], lnb[:, :], "lnb"),
            ]
            for bank, items, dst, name in jobs:
                def tr(e, bank=bank, items=items):
                    last = None
                    off = 0
                    for src_ap, r in items:
                        last = e.transpose(out=pb[bank][:, off:off + r], in_=src_ap, identity=identf[0:r, 0:r])
                        off += r
                    return last
                tot = sum(r for _, r in items)
                P.op("pe", tr, reads=ST_CELLS + ["identf"], writes=[PS(bank)])
                P.op("dve", (lambda e, bank=bank, dst=dst, tot=tot: e.tensor_copy(out=dst, in_=pb[bank][:, 0:tot])),
                     reads=[PS(bank)], writes=[name])

        def mk_consts(e):
            e.memset(onesb[:], 1.0 / 512.0)
            e.memset(epsc[:, 0:1], RMS_EPS)
            return e.memset(epsc[:, 1:2], LN_EPS)
        P.op("pool", mk_consts, writes=["onesb", "epsc"])

        def mk_pads(e):
            e.memset(upad[:], 0.0)
            return e.memset(pp[:], 0.0)
        P.op("pool", mk_pads, writes=[("upad", c) for c in range(4)] + [("pp", s) for s in range(2)])
        par_transposes()

        def load_wout():
            P.dma("pool", lambda e: [e.dma_start(out=wout[:], in_=w_out.rearrange("(c p) d -> p c d", p=128))], 1,
                  "d_wout", writes=["wout"])

        def load_wdn_part(q):
            j0 = 6 * q
            j1 = min(NF, j0 + 6)
            P.dma("pool", (lambda e: [e.dma_start(out=wdn[:, j0:j1, :],
                                                  in_=w_down[j0 * 128:j1 * 128, :].rearrange("(j p) d -> p j d", p=128))]),
                  1, "d_wdn%d" % q, writes=[("wdn", q)])
        WDN_CELLS = [("wdn", q) for q in range(4)]

        win_seq = [(ti, e_) for ti in range(len(tiles)) for e_ in A_ORDER]
        win_state = {"next": 0}

        def prefetch_win():
            n = win_state["next"]
            if n >= len(win_seq):
                return
            win_state["next"] = n + 1
            ti_, e_ = win_seq[n]
            s = n % N_WIN_SLOTS
            if ti_ == 0:
                P.dma("pool", (lambda e, s=s, e_=e_: [e.dma_start(
                    out=winr[:, s, :].rearrange("p (k c) -> p k c", k=KC),
                    in_=w_in[:, e_ * 128:(e_ + 1) * 128].rearrange("(k p) c -> p k c", p=128))]),
                    1, "d_winsw%d" % s, writes=[("winr", s)])
                if len(tiles) > 1:
                    P.dma("sp", (lambda e, s=s, e_=e_: [e.dma_start(out=scr_in[e_], in_=winr[:, s, :])]), 1,
                          "d_wst%d" % s, reads=[("winr", s)], writes=[("scr_in", e_)])
            else:
                P.dma("sp", (lambda e, s=s, e_=e_: [e.dma_start(out=winr[:, s, :], in_=scr_in[e_])]), 1, "d_win%d" % s,
                      reads=[("scr_in", e_)], writes=[("winr", s)])

        gu_total = len(tiles) * NF
        gu_state = {"next": 0}

        def prefetch_gu():
            n = gu_state["next"]
            if n >= gu_total:
                return
            gu_state["next"] = n + 1
            j = n % NF
            s = n % N_GU_SLOTS
            if n < NF:
                def ld(e, s=s, j=j):
                    return [e.dma_start(out=gur[:, s, w_, :].rearrange("p (k c) -> p k c", k=KC),
                                        in_=wsrc[:, j * 128:(j + 1) * 128].rearrange("(k p) c -> p k c", p=128))
                            for w_, wsrc in ((0, w_gate), (1, w_up))]
                P.dma("pool", ld, 2, "d_gusw%d" % s, writes=[("gur", s)])
                if len(tiles) > 1:
                    P.dma("sp", (lambda e, s=s, j=j: [e.dma_start(out=scr_gu[j], in_=gur[:, s, :, :])]), 1,
                          "d_gst%d" % s, reads=[("gur", s)], writes=[("scr_gu", j)])
            else:
                P.dma("sp", (lambda e, s=s, j=j: [e.dma_start(out=gur[:, s, :, :], in_=scr_gu[j])]), 1, "d_gu%d" % s,
                      reads=[("scr_gu", j)], writes=[("gur", s)])


        ctr = {"xa": 0, "stat": 0, "tp": 0, "ev": 0, "diag": 0, "hb": 0, "win": 0, "gu": 0, "c1": 0, "tln": 0}

        def nxt(name, mod):
            v = ctr[name]
            ctr[name] = v + 1
            return v % mod

        def rms_chain(src_ap, src_cells, m, junk_ap, junk_cells, eps_col):
            sc = nxt("stat", NSTAT)
            cell = ("stat", sc)
            P.op("act", (lambda e: e.activation(out=junk_ap, in_=src_ap, func=AF.Square, scale=1.0 / 32.0,
                                                accum_out=stat[0:m, sc, 0:1])),
                 reads=src_cells, writes=list(junk_cells) + [cell])
            P.op("act", (lambda e: e.activation(out=stat[0:m, sc, 1:2], in_=stat[0:m, sc, 0:1], func=AF.Sqrt,
                                                bias=epsc[0:m, eps_col:eps_col + 1], scale=1.0)),
                 reads=[cell, "epsc"], writes=[cell])
            P.op("dve", (lambda e: e.reciprocal(out=stat[0:m, sc, 2:3], in_=stat[0:m, sc, 1:2])),
                 reads=[cell], writes=[cell])
            return stat[0:m, sc, 2:3], cell

        def norm_a(src_ap, src_cells, m, buf_ap, buf_cell):
            rs, rcell = rms_chain(src_ap, src_cells, m, buf_ap, [buf_cell], 0)
            P.op("act", (lambda e: e.activation(out=buf_ap, in_=src_ap, func=AF.Copy, scale=rs)),
                 reads=list(src_cells) + [rcell], writes=[buf_cell])

        def norm_b(buf3, buf_cell, m, dstT, dst_cells, col0, gcol, gname):
            tb = nxt("tp", 2)

            def tr(e):
                last = None
                for k in range(KC):
                    last = e.transpose(out=tp[tb][:, k, 0:m], in_=buf3[:, k, :], identity=identb[0:m, 0:m])
                return last
            P.op("pe", tr, reads=[buf_cell, "identb"], writes=[PS(6 + tb)])
            P.op("dve", (lambda e: e.tensor_tensor(out=dstT[:, :, col0:col0 + m], in0=tp[tb][:, :, 0:m],
                                                   in1=gcol[:, 0:KC].unsqueeze(2).broadcast_to([128, KC, m]),
                                                   op=ALU.mult)),
                 reads=[PS(6 + tb), gname], writes=dst_cells)

        def s0_geom(ti):
            t0, T = tiles[ti]
            a0 = max(0, t0 - HALO)
            a1 = min(S, t0 + T + HALO)
            nA = a1 - a0
            return a0, nA, (nA + 127) // 128

        def s0_load(ti, c):
            a0, nA, nch = s0_geom(ti)
            if c >= nch:
                return
            m = min(128, nA - 128 * c)
            s = c % 2
            r0 = a0 + 128 * c
            P.dma("sp", (lambda e, s=s, r0=r0, m=m: [e.dma_start(out=xa[0:m, s, :], in_=x[r0:r0 + m, :])]), 1,
                  "d_xa%d" % s, writes=[("xa", s)])

        def s0_norm(ti, c):
            a0, nA, nch = s0_geom(ti)
            if c >= nch:
                return
            m = min(128, nA - 128 * c)
            s = c % 2
            norm_a(xa[0:m, s, :], [("xa", s)], m, hbs[0:m, c, :], ("hbs", c))

        def stage_S0b(ti, c):
            a0, nA, nch = s0_geom(ti)
            if c >= nch:
                return
            m = min(128, nA - 128 * c)
            norm_b(hbs[0:m, c, :].rearrange("p (k c) -> p k c", k=KC), ("hbs", c), m, hT, [("hT", c)], 128 * c,
                   gmix, "gmix")

        def stage_mixer(ti):
            t0, T = tiles[ti]
            a0 = max(0, t0 - HALO)
            a1 = min(S, t0 + T + HALO)
            nA = a1 - a0
            b0 = max(0, t0 - 1)
            b1 = min(S, t0 + T + 1)
            nB = b1 - b0
            oB = b0 - a0
            ncA = (nA + 127) // 128
            ntc = (nB + 127) // 128

            for tc in range(ntc):
                m = min(128, nB - 128 * tc)
                r0 = b0 + 128 * tc
                P.dma("pool", (lambda e, tc=tc, r0=r0, m=m: [e.dma_start(out=x1[0:m, tc, :], in_=x[r0:r0 + m, :])]), 1,
                      "d_x1_%d" % tc, writes=[("x1", tc)])

            hT_cells = [("hT", c) for c in range(ncA)]
            abank = {"n": 0}
            evslot = {}
            ctslot = {}
            ppslot = {}

            if nA < WIN:
                P.op("pool", (lambda e: e.memset(upad[:, :, 15 + nA:UPW], 0.0)),
                     writes=[("upad", c) for c in range(4)])

            def a_group(e_):
                ws = nxt("win", N_WIN_SLOTS)
                bank = abank["n"] % 4
                abank["n"] += 1

                def mmA(e, ws=ws, bank=bank):
                    last = None
                    for k in range(KC):
                        last = e.matmul(pb[bank][:, 0:nA], lhsT=winr[:, ws, k * 128:(k + 1) * 128], rhs=hT[:, k, 0:nA],
                                        start=(k == 0), stop=(k == KC - 1))
                    return last
                P.op("pe", mmA, reads=[("winr", ws)] + hT_cells, writes=[PS(bank)])
                prefetch_win()
                if ti == 0 and e_ == A_ORDER[-5]:
                    load_wout()
                    for _ in range(N_GU_SLOTS):
                        prefetch_gu()
                    for q in range(4):
                        load_wdn_part(q)
                kind, c = e_ // 4, e_ % 4
                if kind == 4:
                    s = nxt("ev", 3)
                    evslot[("g", c)] = s
                    P.op("act", (lambda e, s=s, bank=bank: e.activation(out=ev[:, s, 0:nA], in_=pb[bank][:, 0:nA],
                                                                         func=AF.Tanh, scale=0.5)),
                         reads=[PS(bank)], writes=[("ev", s)])
                elif kind == 3:
                    s = evslot[("g", c)]
                    P.op("dve", (lambda e, s=s, bank=bank, c=c: e.scalar_tensor_tensor(
                        out=upad[:, c, 15:15 + nA], in0=ev[:, s, 0:nA], scalar=1.0, in1=pb[bank][:, 0:nA],
                        op0=ALU.add, op1=ALU.mult)),
                        reads=[("ev", s), PS(bank)], writes=[("upad", c)])
                elif kind == 0:
                    s = nxt("ev", 3)
                    evslot[("h", c)] = s
                    P.op("act", (lambda e, s=s, bank=bank: e.activation(out=ev[:, s, 0:nA], in_=pb[bank][:, 0:nA],
                                                                         func=AF.Copy)),
                         reads=[PS(bank)], writes=[("ev", s)])
                elif kind == 2:
                    s = evslot[("h", c)]
                    ps_ = c % 2
                    cs = c
                    ppslot[c] = ps_
                    ctslot[c] = cs
                    P.op("dve", (lambda e, s=s, bank=bank, ps_=ps_: e.tensor_tensor(
                        out=pp[:, ps_, 1:1 + nA], in0=pb[bank][:, 0:nA], in1=ev[:, s, 0:nA], op=ALU.mult)),
                        reads=[("ev", s), PS(bank)], writes=[("pp", ps_)])
                    if nA < WIN:
                        P.op("pool", (lambda e, ps_=ps_: e.memset(pp[:, ps_, 1 + nA:PPW], 0.0)), writes=[("pp", ps_)])
                    P.op("act", (lambda e, ps_=ps_, cs=cs, c=c: e.activation(
                        out=ct[:, cs, 0:nB], in_=pp[:, ps_, 1 + oB:1 + oB + nB], func=AF.Copy, scale=caw[:, c, 1:2])),
                        reads=[("pp", ps_), "caw"], writes=[("ct", cs)])
                    P.op("dve", (lambda e, ps_=ps_, cs=cs, c=c: e.scalar_tensor_tensor(
                        out=ct[:, cs, 0:nB], in0=pp[:, ps_, oB:oB + nB], scalar=caw[:, c, 0:1], in1=ct[:, cs, 0:nB],
                        op0=ALU.mult, op1=ALU.add)), reads=[("pp", ps_), ("ct", cs), "caw"], writes=[("ct", cs)])
                    P.op("dve", (lambda e, ps_=ps_, cs=cs, c=c: e.scalar_tensor_tensor(
                        out=ct[:, cs, 0:nB], in0=pp[:, ps_, 2 + oB:2 + oB + nB], scalar=caw[:, c, 2:3],
                        in1=ct[:, cs, 0:nB], op0=ALU.mult, op1=ALU.add)),
                        reads=[("pp", ps_), ("ct", cs), "caw"], writes=[("ct", cs)])
                elif kind == 1:
                    cs = ctslot[c]
                    P.op("dve", (lambda e, cs=cs, bank=bank, c=c: e.tensor_tensor(
                        out=yT[:, c, 0:nB], in0=pb[bank][:, oB:oB + nB], in1=ct[:, cs, 0:nB], op=ALU.mult)),
                        reads=[("ct", cs), PS(bank)], writes=[("yT", c)])

            diag_q = []

            def gen_diag(idx):
                c, kb = idx // 4, idx % 4
                k0 = 8 * kb
                nt = min(8, 31 - k0)
                ds = nxt("diag", N_DIAG_BLK)
                P.op(DIAG_ENGINE, (lambda e, ds=ds, c=c, k0=k0, nt=nt: e.tensor_tensor(
                    out=diag[:, ds, 0:nt, :], in0=identh[:].unsqueeze(1).broadcast_to([128, nt, 128]),
                    in1=cbw[:, c, k0:k0 + nt].unsqueeze(2).broadcast_to([128, nt, 128]), op=ALU.mult)),
                    reads=["identh", "cbw"], writes=[("diag", ds)])
                diag_q.append(ds)

            diag_state = {"gen": 0}

            def ensure_diag(upto):
                while diag_state["gen"] <= upto and diag_state["gen"] < 16:
                    gen_diag(diag_state["gen"])
                    diag_state["gen"] += 1

            def conv_chunk(c):
                bank = 4 + (c % 2)
                for kb in range(4):
                    k0 = 8 * kb
                    nt = min(8, 31 - k0)
                    idx = 4 * c + kb
                    ensure_diag(idx)
                    ds = diag_q[idx]

                    def mmC(e, ds=ds, c=c, k0=k0, nt=nt, bank=bank):
                        last = None
                        for t in range(nt):
                            k = k0 + t
                            last = e.matmul(pb[bank][:, 0:nB], lhsT=diag[:, ds, t, :],
                                            rhs=upad[:, c, oB + k:oB + k + nB], start=(k == 0), stop=(k == 30))
                        return last
                    P.op("pe", mmC, reads=[("diag", ds), ("upad", c)], writes=[PS(bank)])
                    ensure_diag(min(15, idx + N_DIAG_BLK - 1))
                P.op("act", (lambda e, c=c, bank=bank: e.activation(out=ucbv(c)[:, 0:nB], in_=pb[bank][:, 0:nB],
                                                                     func=AF.Identity, bias=cbb[:, c:c + 1], scale=1.0)),
                     reads=[PS(bank), "cbb"], writes=ucb_cells(c))
                P.op("act", (lambda e, c=c, bank=bank: e.activation(out=usqv(c)[:, 0:nB], in_=pb[bank][:, 0:nB],
                                                                     func=AF.Square, bias=cbb[:, c:c + 1], scale=1.0)),
                     reads=[PS(bank), "cbb"], writes=usq_cells(c))
                P.op("act", (lambda e, c=c, bank=bank: e.activation(out=ucv(c)[:, 0:nB], in_=pb[bank][:, 0:nB],
                                                                     func=AF.Identity, bias=cbb[:, c:c + 1], scale=1.0)),
                     reads=[PS(bank), "cbb"], writes=uc_cells(c))

            def stat_chunk(c):
                P.op("pe", (lambda e, c=c: e.matmul(stat_ps[0][:, 0:nB], lhsT=onesb[:], rhs=ucbv(c)[:, 0:nB],
                                                     start=(c == 0), stop=(c == 3))),
                     reads=["onesb"] + ucb_cells(c), writes=[PS(6)])
                P.op("pe", (lambda e, c=c: e.matmul(stat_ps[1][:, 0:nB], lhsT=onesb[:], rhs=usqv(c)[:, 0:nB],
                                                     start=(c == 0), stop=(c == 3))),
                     reads=["onesb"] + usq_cells(c), writes=[PS(7)])

            ensure_diag(N_DIAG_BLK - 2)
            for e_ in A_ORDER[:8]:
                a_group(e_)
            conv_chunk(0)
            conv_chunk(1)
            stat_chunk(0)
            conv_chunk(2)
            stat_chunk(1)
            conv_chunk(3)
            stat_chunk(2)
            stat_chunk(3)

            L = []
            L.append(lambda: P.op("act", (lambda e: e.activation(out=mean_sb[:, 0:nB], in_=stat_ps[0][:, 0:nB],
                                                                 func=AF.Copy)), reads=[PS(6)], writes=mean_cells))
            L.append(lambda: P.op("act", (lambda e: e.activation(out=var_sb[:, 0:nB], in_=stat_ps[0][:, 0:nB],
                                                                 func=AF.Square)), reads=[PS(6)], writes=var_cells))
            L.append(lambda: P.op("dve", (lambda e: e.tensor_tensor(out=var_sb[:, 0:nB], in0=stat_ps[1][:, 0:nB],
                                                                    in1=var_sb[:, 0:nB], op=ALU.subtract)),
                                  reads=[PS(7)] + var_cells, writes=var_cells))
            L.append(lambda: P.op("act", (lambda e: e.activation(out=var_sb[:, 0:nB], in_=var_sb[:, 0:nB], func=AF.Sqrt,
                                                                 bias=epsc[:, 1:2], scale=1.0)),
                                  reads=var_cells + ["epsc"], writes=var_cells))
            L.append(lambda: P.op("dve", (lambda e: e.reciprocal(out=var_sb[:, 0:nB], in_=var_sb[:, 0:nB])),
                                  reads=var_cells, writes=var_cells))
            LN_mul, LN_sub, LN_silu = [], [], []
            for c in range(4):
                eng_c = "dve" if c % 2 == 0 else LN_ENGINE
                LN_sub.append(lambda c=c, eng_c=eng_c: P.op(eng_c, (lambda e: e.tensor_tensor(
                    out=ucv(c)[:, 0:nB], in0=ucv(c)[:, 0:nB], in1=mean_sb[:, 0:nB], op=ALU.subtract)),
                    reads=uc_cells(c) + mean_cells, writes=uc_cells(c)))
                LN_mul.append(lambda c=c, eng_c=eng_c: P.op(eng_c, (lambda e: e.tensor_tensor(
                    out=ucv(c)[:, 0:nB], in0=ucv(c)[:, 0:nB], in1=var_sb[:, 0:nB], op=ALU.mult)),
                    reads=uc_cells(c) + var_cells, writes=uc_cells(c)))
                LN_silu.append(lambda c=c: P.op("act", (lambda e: e.activation(
                    out=yT[:, 4 + c, 0:nB], in_=ucv(c)[:, 0:nB], func=AF.Silu, scale=lng[:, c:c + 1],
                    bias=lnb[:, c:c + 1])), reads=uc_cells(c) + ["lng", "lnb"], writes=[("yT", 4 + c)]))
            sched = {
                0: [L[0], L[1]],
                1: [L[2], LN_sub[0], LN_sub[1]],
                2: [L[3], LN_sub[2], LN_sub[3]],
                4: [L[4]],
                5: [LN_mul[0], LN_mul[1]],
                6: [LN_mul[2], LN_mul[3]],
                8: [LN_silu[0], LN_silu[1]],
                9: [LN_silu[2], LN_silu[3]],
            }
            for gi, e_ in enumerate(A_ORDER[8:]):
                a_group(e_)
                for th in sched.get(gi, []):
                    th()

            yT_cells = [("yT", c) for c in range(8)]
            obank = {"n": 0}
            pend_b = None
            for tc in range(ntc):
                m = min(128, nB - 128 * tc)
                for dh in range(2):
                    bank = 4 + (obank["n"] % 2)
                    obank["n"] += 1

                    def mmO(e, tc=tc, m=m, dh=dh, bank=bank):
                        last = None
                        for cc in range(KC):
                            last = e.matmul(pb[bank][0:m, :], lhsT=yT[:, cc, 128 * tc:128 * tc + m],
                                            rhs=wout[:, cc, dh * 512:(dh + 1) * 512], start=(cc == 0), stop=(cc == KC - 1))
                        return last
                    P.op("pe", mmO, reads=yT_cells + ["wout"], writes=[PS(bank)])
                    P.op("dve", (lambda e, tc=tc, m=m, dh=dh, bank=bank: e.tensor_tensor(
                        out=x1[0:m, tc, dh * 512:(dh + 1) * 512], in0=pb[bank][0:m, :],
                        in1=x1[0:m, tc, dh * 512:(dh + 1) * 512], op=ALU.add)),
                        reads=[PS(bank), ("x1", tc)], writes=[("x1", tc)])
                hs = nxt("hb", 3)
                norm_a(x1[0:m, tc, :], [("x1", tc)], m, hbf[0:m, hs, :], ("hbf", hs))
                if pend_b is not None:
                    pend_b()

                def pb_(tc=tc, m=m, hs=hs):
                    norm_b(hbf[0:m, hs, :].rearrange("p (k c) -> p k c", k=KC), ("hbf", hs), m, h2T, [("h2T", tc)],
                           128 * tc, gffn, "gffn")
                pend_b = pb_
            return nB, ntc, b0, pend_b

        def stage_ffn(ti, nB, ntc, b0, hoist, pend_b):
            t0, T = tiles[ti]
            h2T_cells = [("h2T", tc) for tc in range(ntc)]
            pending_mul = None
            tails = []
            evq = []
            if ntc <= 1:
                pend_b()
            for j in range(NF):
                gs = nxt("gu", N_GU_SLOTS)
                if j < 18:
                    gb = (0, 1, 2)[j % 3]
                    vb = (4, 5, 3)[j % 3]
                else:
                    gb = (0, 1, 6)[j % 3]
                    vb = (4, 5, 7)[j % 3]

                def mk_mm(w_, bank, c0, c1, gs=gs):
                    def mm(e):
                        last = None
                        for k in range(KC):
                            last = e.matmul(pbv(bank)[:, c0:c1], lhsT=gur[:, gs, w_, k * 128:(k + 1) * 128],
                                            rhs=h2T[:, k, c0:c1], start=(k == 0), stop=(k == KC - 1))
                        return last
                    return mm
                if j < 2 and ntc > 1:
                    c_head = 128 * max(1, ntc - 2)
                    pieces = [(0, c_head)]
                    if ntc > 2:
                        pieces.append((c_head, 128 * (ntc - 1)))
                    pieces.append((128 * (ntc - 1), nB))
                    hd = h2T_cells[:max(1, ntc - 2)]
                    P.op("pe", mk_mm(0, gb, 0, c_head), reads=[("gur", gs)] + hd, writes=[PS(gb)])
                    P.op("pe", mk_mm(1, vb, 0, c_head), reads=[("gur", gs)] + hd, writes=[PS(vb)])
                    tails.append((gs, gb, vb))
                    if j == 1:
                        pend_b()
                        for pi_, (c0_, c1_) in enumerate(pieces[1:]):
                            need = h2T_cells[:(c1_ + 127) // 128]
                            for (gs_, gb_, vb_) in tails:
                                P.op("pe", mk_mm(0, gb_, c0_, c1_, gs_), reads=[("gur", gs_)] + need, writes=[PS(gb_)])
                                P.op("pe", mk_mm(1, vb_, c0_, c1_, gs_), reads=[("gur", gs_)] + need, writes=[PS(vb_)])
                else:
                    P.op("pe", mk_mm(0, gb, 0, nB), reads=[("gur", gs)] + h2T_cells, writes=[PS(gb)])
                    P.op("pe", mk_mm(1, vb, 0, nB), reads=[("gur", gs)] + h2T_cells, writes=[PS(vb)])
                if j < 2 and ntc > 1:
                    if j == 1:
                        prefetch_gu()
                        prefetch_gu()
                else:
                    prefetch_gu()
                def evac(j=j, gb=gb, vb=vb):
                    nonlocal pending_mul
                    cs = nxt("c1", 2)
                    cc_ = c1_cells(cs)
                    P.op("act", (lambda e, cs=cs, gb=gb, j=j: e.activation(out=c1v(cs)[:, 0:nB], in_=pbv(gb)[:, 0:nB],
                                                                            func=AF.Copy, scale=cfw[:, j, 1:2])),
                         reads=[PS(gb), "cfw"], writes=cc_)
                    P.op("dve", (lambda e, cs=cs, gb=gb, j=j: e.scalar_tensor_tensor(
                        out=c1v(cs)[:, 1:nB], in0=pbv(gb)[:, 0:nB - 1], scalar=cfw[:, j, 0:1], in1=c1v(cs)[:, 1:nB],
                        op0=ALU.mult, op1=ALU.add)), reads=[PS(gb), "cfw"] + cc_, writes=cc_)
                    if pending_mul is not None:
                        pending_mul()
                    P.op("dve", (lambda e, cs=cs, gb=gb, j=j: e.scalar_tensor_tensor(
                        out=c1v(cs)[:, 0:nB - 1], in0=pbv(gb)[:, 1:nB], scalar=cfw[:, j, 2:3], in1=c1v(cs)[:, 0:nB - 1],
                        op0=ALU.mult, op1=ALU.add)), reads=[PS(gb), "cfw"] + cc_, writes=cc_)
                    P.op("act", (lambda e, cs=cs: e.activation(out=c1v(cs)[:, 0:nB], in_=c1v(cs)[:, 0:nB], func=AF.Silu)),
                         reads=cc_, writes=cc_)

                    def mul(cs=cs, vb=vb, j=j, cc_=cc_):
                        P.op("dve", (lambda e: e.tensor_tensor(out=aT(j)[:, 0:nB], in0=pbv(vb)[:, 0:nB],
                                                               in1=c1v(cs)[:, 0:nB], op=ALU.mult)),
                             reads=[PS(vb)] + cc_, writes=aT_cells(j))
                    pending_mul = mul
                if j < 2 and ntc > 1:
                    evq.append(evac)
                    if j == 1:
                        for ev_ in evq:
                            ev_()
                else:
                    evac()
            pending_mul()

            aT_all = [c for j in range(NF) for c in aT_cells(j)]
            dbank = {"n": 0}
            if hoist is not None:
                s0_load(hoist, 0)
                s0_load(hoist, 1)
            for tc in range(ntc):
                m = min(128, nB - 128 * tc)
                for dh in range(2):
                    bank = 2 + (dbank["n"] % 2)
                    dbank["n"] += 1

                    def mmD(e, tc=tc, m=m, dh=dh, bank=bank):
                        last = None
                        for j in range(NF):
                            last = e.matmul(pb[bank][0:m, :], lhsT=aT(j)[:, 128 * tc:128 * tc + m],
                                            rhs=wdn[:, j, dh * 512:(dh + 1) * 512], start=(j == 0), stop=(j == NF - 1))
                        return last
                    if dbank["n"] == 1:
                        def mmD1(e, tc=tc, m=m, dh=dh, bank=bank):
                            last = None
                            for j in range(NF - 4):
                                last = e.matmul(pb[bank][0:m, :], lhsT=aT(j)[:, 128 * tc:128 * tc + m],
                                                rhs=wdn[:, j, dh * 512:(dh + 1) * 512], start=(j == 0), stop=False)
                            return last

                        def mmD2(e, tc=tc, m=m, dh=dh, bank=bank):
                            last = None
                            for j in range(NF - 4, NF):
                                last = e.matmul(pb[bank][0:m, :], lhsT=aT(j)[:, 128 * tc:128 * tc + m],
                                                rhs=wdn[:, j, dh * 512:(dh + 1) * 512], start=False, stop=(j == NF - 1))
                            return last
                        head_cells = [c for j in range(NF - 4) for c in aT_cells(j)]
                        P.op("pe", mmD1, reads=head_cells + WDN_CELLS, writes=[PS(bank)])
                        P.op("pe", mmD2, reads=aT_all + WDN_CELLS, writes=[PS(bank)])
                    else:
                        P.op("pe", mmD, reads=aT_all + WDN_CELLS, writes=[PS(bank)])
                    if hoist is not None:
                        g_ = dbank["n"] - 1
                        if g_ == 0:
                            s0_norm(hoist, 0)
                            s0_load(hoist, 2)
                        elif g_ == 1:
                            s0_norm(hoist, 1)
                            s0_load(hoist, 3)
                        elif g_ == 2:
                            stage_S0b(hoist, 0)
                            s0_norm(hoist, 2)
                        elif g_ == 3:
                            stage_S0b(hoist, 1)
                            s0_norm(hoist, 3)
                        elif g_ == 4:
                            stage_S0b(hoist, 2)
                        elif g_ == 5:
                            stage_S0b(hoist, 3)
                    P.op("dve", (lambda e, tc=tc, m=m, dh=dh, bank=bank: e.tensor_tensor(
                        out=x1[0:m, tc, dh * 512:(dh + 1) * 512], in0=pb[bank][0:m, :],
                        in1=x1[0:m, tc, dh * 512:(dh + 1) * 512], op=ALU.add)),
                        reads=[PS(bank), ("x1", tc)], writes=[("x1", tc)])
                hs = nxt("hb", 3)
                rs, rcell = rms_chain(x1[0:m, tc, :], [("x1", tc)], m, hbf[0:m, hs, :], [("hbf", hs)], 0)
                P.op("dve", (lambda e, tc=tc, m=m, rs=rs: e.scalar_tensor_tensor(
                    out=x1[0:m, tc, :], in0=x1[0:m, tc, :], scalar=rs, in1=gfin[0:m, :], op0=ALU.mult, op1=ALU.mult)),
                    reads=[("x1", tc), rcell, "gfin"], writes=[("x1", tc)])
                r_lo = max(0, (t0 - b0) - 128 * tc)
                r_hi = min(m, (t0 + T - b0) - 128 * tc)
                if r_lo < r_hi:
                    g0 = b0 + 128 * tc
                    tok = P.dma("pool", (lambda e, tc=tc, r_lo=r_lo, r_hi=r_hi, g0=g0: [e.dma_start(
                        out=y[g0 + r_lo:g0 + r_hi, :], in_=x1[r_lo:r_hi, tc, :])]), 1, "d_x1_%d" % tc,
                        reads=[("x1", tc)])
                    P.final_tokens.append(tok)

        s0_load(0, 0)
        s0_load(0, 1)
        s0_norm(0, 0)
        s0_load(0, 2)
        s0_norm(0, 1)
        s0_load(0, 3)
        for _ in range(N_WIN_SLOTS):
            prefetch_win()
        stage_S0b(0, 0)
        stage_S0b(0, 1)
        s0_norm(0, 2)
        stage_S0b(0, 2)
        s0_norm(0, 3)
        stage_S0b(0, 3)
        for ti in range(len(tiles)):
            nB, ntc, b0, pend_b = stage_mixer(ti)
            hoist = ti + 1 if ti + 1 < len(tiles) else None
            stage_ffn(ti, nB, ntc, b0, hoist, pend_b)

        P.emit(nc, es)
    return nc


_PARAM_KEYS = ["norm_mix_g", "w_in", "conv_a_w", "conv_b_w", "conv_b_b", "ln_b_g", "ln_b_b", "w_out",
               "norm_ffn_g", "w_gate", "w_up", "conv_ffn_w", "w_down", "norm_final_g"]


def _prep_params(inputs):
    out = {}
    for k in _PARAM_KEYS:
        a = np.asarray(inputs[k], dtype=np.float32)
        if k != "norm_final_g":
            a = a[0]
        out[k] = np.ascontiguousarray(a)
    return out


def kernel(**inputs):
    x = np.asarray(inputs["x"], dtype=np.float32)
    B, S, _ = x.shape
    params = _prep_params(inputs)
    nc = build(S)
    in_maps = []
    for b in range(B):
        m = dict(params)
        m["x"] = np.ascontiguousarray(x[b])
        in_maps.append(m)
    res = run_bass_kernel_spmd(nc, in_maps, core_ids=list(range(B)))
    return np.stack([np.asarray(r["y"], dtype=np.float32) for r in res.results], axis=0)
```

```python
import numpy as np
from contextlib import ExitStack

import concourse.bass as bass
import concourse.mybir as mybir
from concourse.bass_utils import run_bass_kernel_spmd

F32 = mybir.dt.float32
BF16 = mybir.dt.bfloat16
AF = mybir.ActivationFunctionType
ALU = mybir.AluOpType

D = 1024
DA = 512
DB = 512
DIN = 2560
DFF = 2816
NF = DFF // 128
KC = D // 128
NE = DIN // 128
RMS_EPS = 1e-6
LN_EPS = 1e-5
HALO = 16
WIN = 512

A_ORDER = [16, 12, 17, 13, 18, 14, 19, 15, 0, 8, 1, 9, 2, 10, 3, 11, 4, 5, 6, 7]

N_WIN_SLOTS = 5
N_GU_SLOTS = 3
N_DIAG_BLK = 3
DIAG_ENGINE = "dve"
LN_ENGINE = "pool"


def plan_tiles(S):
    tiles = []
    t0 = 0
    while t0 < S:
        if t0 == 0:
            T = min(S, WIN - HALO)
        else:
            T = min(WIN - 2 * HALO, S - t0)
            if S - t0 <= WIN - HALO:
                T = S - t0
        tiles.append((t0, T))
        t0 += T
    return tiles


class Prog:
    COMPUTE = ("act", "dve", "pool", "pe")

    def __init__(self):
        self.streams = {e: [] for e in ("sp", "act", "dve", "pool", "pe")}
        self.ccount = {e: 0 for e in self.COMPUTE}
        self.dcount = {}
        self.state = {}
        self.waited = {e: {} for e in self.streams}
        self.semnames = set("c_" + e for e in self.COMPUTE)
        self.final_tokens = []

    def _deps(self, eng, reads, writes, is_dma):
        need = {}

        def add(tok, raw):
            sem, val, src = tok
            if src == eng and not is_dma:
                if eng == "pe":
                    return
                if not raw:
                    return
            if need.get(sem, 0) < val:
                need[sem] = val

        for c in reads:
            st = self.state.get(c)
            if st is not None and st[0] is not None:
                add(st[0], True)
        for c in writes:
            st = self.state.get(c)
            if st is not None:
                if st[0] is not None:
                    add(st[0], False)
                for t in st[1].values():
                    add(t, False)
        waits = []
        wd = self.waited[eng]
        for sem, val in need.items():
            if wd.get(sem, 0) < val:
                wd[sem] = val
                waits.append((sem, val))
        return waits

    def _commit(self, tok, reads, writes):
        for c in reads:
            st = self.state.get(c)
            if st is None:
                st = [None, {}]
                self.state[c] = st
            st[1][(tok[0], tok[2])] = tok
        for c in writes:
            self.state[c] = [tok, {}]

    def op(self, eng, fn, reads=(), writes=()):
        reads = tuple(reads)
        writes = tuple(writes)
        waits = self._deps(eng, reads, writes, False)
        self.ccount[eng] += 1
        tok = ("c_" + eng, self.ccount[eng], eng)
        self.streams[eng].append((waits, fn, "c_" + eng, 1))
        self._commit(tok, reads, writes)
        return tok

    def dma(self, queue, fn, n, sem, reads=(), writes=()):
        reads = tuple(reads)
        writes = tuple(writes)
        waits = self._deps(queue, reads, writes, True)
        self.semnames.add(sem)
        self.dcount[sem] = self.dcount.get(sem, 0) + n
        tok = (sem, 16 * self.dcount[sem], queue)
        self.streams[queue].append((waits, fn, sem, 16))
        self._commit(tok, reads, writes)
        return tok

    def emit(self, nc, es):
        sems = {name: es.enter_context(nc.semaphore(name)) for name in sorted(self.semnames)}
        block = es.enter_context(nc.Block())
        streams = self.streams
        finals = self.final_tokens

        awaited = set()
        for name in streams:
            for waits, fn, sem, inc in streams[name]:
                for s, v in waits:
                    awaited.add((s, v))
        remap = {}
        for name in self.COMPUTE:
            sem = "c_" + name
            new = 0
            idx = 0
            m = {}
            for waits, fn, s_, inc in streams[name]:
                if inc != 1:
                    continue
                idx += 1
                if (sem, idx) in awaited:
                    new += 1
                    m[idx] = new
            remap[sem] = m

        def run(eng, name):
            idx = 0
            for waits, fn, sem, inc in streams[name]:
                for s, v in waits:
                    if s in remap:
                        v = remap[s][v]
                    eng.wait_ge(sems[s], v)
                res = fn(eng)
                if inc == 16:
                    for ins in res:
                        ins.then_inc(sems[sem], 16)
                else:
                    idx += 1
                    if idx in remap[sem]:
                        last = res[-1] if isinstance(res, (list, tuple)) else res
                        last.then_inc(sems[sem], 1)
            if name == "sp":
                done = {}
                for s, v, _ in finals:
                    done[s] = max(done.get(s, 0), v)
                for s, v in done.items():
                    eng.wait_ge(sems[s], v)

        @block.sync
        def _(e):
            run(e, "sp")

        @block.scalar
        def _(e):
            run(e, "act")

        @block.vector
        def _(e):
            run(e, "dve")

        @block.gpsimd
        def _(e):
            run(e, "pool")

        @block.tensor
        def _(e):
            run(e, "pe")


def arcells(lo, hi):
    return [("ar", i) for i in range(lo // 1024, (hi - 1) // 1024 + 1)]


def build(S):
    nc = bass.Bass("TRN2", target_bir_lowering=False)
    dt = nc.dram_tensor
    x = dt("x", [S, D], F32, kind="ExternalInput").ap()
    norm_mix_g = dt("norm_mix_g", [D], F32, kind="ExternalInput").ap()
    w_in = dt("w_in", [D, DIN], F32, kind="ExternalInput").ap()
    conv_a_w = dt("conv_a_w", [3, DA], F32, kind="ExternalInput").ap()
    conv_b_w = dt("conv_b_w", [31, DB], F32, kind="ExternalInput").ap()
    conv_b_b = dt("conv_b_b", [DB], F32, kind="ExternalInput").ap()
    ln_b_g = dt("ln_b_g", [DB], F32, kind="ExternalInput").ap()
    ln_b_b = dt("ln_b_b", [DB], F32, kind="ExternalInput").ap()
    w_out = dt("w_out", [D, D], F32, kind="ExternalInput").ap()
    norm_ffn_g = dt("norm_ffn_g", [D], F32, kind="ExternalInput").ap()
    w_gate = dt("w_gate", [D, DFF], F32, kind="ExternalInput").ap()
    w_up = dt("w_up", [D, DFF], F32, kind="ExternalInput").ap()
    conv_ffn_w = dt("conv_ffn_w", [3, DFF], F32, kind="ExternalInput").ap()
    w_down = dt("w_down", [DFF, D], F32, kind="ExternalInput").ap()
    norm_final_g = dt("norm_final_g", [D], F32, kind="ExternalInput").ap()
    y = dt("y", [S, D], F32, kind="ExternalOutput").ap()
    scr_in = dt("scr_in", [NE, 128, 1024], BF16, kind="Internal").ap()
    scr_gu = dt("scr_gu", [NF, 128, 2, 1024], BF16, kind="Internal").ap()

    tiles = plan_tiles(S)
    P = Prog()

    with ExitStack() as es:
        def sb(name, shape, dtype):
            return es.enter_context(nc.sbuf_tensor(name, shape, dtype))

        wout = sb("wout", [128, KC, D], BF16)
        wdn = sb("wdn", [128, NF, D], BF16)
        winr = sb("winr", [128, N_WIN_SLOTS, 1024], BF16)
        gur = sb("gur", [128, N_GU_SLOTS, 2, 1024], BF16)
        diag = sb("diag", [128, N_DIAG_BLK, 8, 128], BF16)
        gfin = sb("gfin", [128, D], F32)
        identf = sb("identf", [128, 128], F32)
        identh = sb("identh", [128, 128], F32)
        identb = sb("identb", [128, 128], BF16)
        onesb = sb("onesb", [128, 128], BF16)
        gmix = sb("gmix", [128, KC], F32)
        gffn = sb("gffn", [128, KC], F32)
        caw = sb("caw", [128, 4, 3], F32)
        cbw = sb("cbw", [128, 4, 31], F32)
        cbb = sb("cbb", [128, 4], F32)
        lng = sb("lng", [128, 4], F32)
        lnb = sb("lnb", [128, 4], F32)
        cfw = sb("cfw", [128, NF, 3], F32)
        epsc = sb("epsc", [128, 2], F32)
        xa = sb("xa", [128, 2, D], F32)
        hbf = sb("hbf", [128, 3, D], BF16)
        hbs = sb("hbs", [128, 4, D], BF16)
        hT = sb("hT", [128, KC, WIN], BF16)
        h2T = sb("h2T", [128, KC, WIN], BF16)
        yT = sb("yT", [128, KC, WIN], BF16)
        x1 = sb("x1", [128, 4, D], F32)
        NSTAT = 16
        stat = sb("stat", [128, NSTAT, 4], F32)
        ev = sb("ev", [128, 3, WIN], F32)
        UPW = 544
        upad = sb("upad", [128, 4, UPW], BF16)
        PPW = 516
        pp = sb("pp", [128, 2, PPW], F32)
        ct = sb("ct", [128, 4, WIN], F32)
        ARENA_E = 13312
        arena = sb("arena", [128, ARENA_E], BF16)

        pb = [es.enter_context(nc.psum_tensor("pb%d" % i, [128, 512], F32)) for i in range(6)]
        tp = [es.enter_context(nc.psum_tensor("tp%d" % i, [128, KC, 128], BF16)) for i in range(2)]

        def PS(b):
            return ("ps", b)

        stat_ps = [tp[b][:, :, :].rearrange("p k c -> p (k c)").bitcast(F32) for b in range(2)]

        def pbv(b):
            return pb[b] if b < 6 else stat_ps[b - 6]

        def ar_bf(lo_b, n):
            return arena[:, lo_b // 2: lo_b // 2 + n]

        def ar_f32(lo_b, n):
            return arena[:, lo_b // 2: lo_b // 2 + 2 * n].bitcast(F32)

        def aT(j):
            return ar_bf(j * 1024, 512)

        def aT_cells(j):
            return arcells(j * 1024, (j + 1) * 1024)

        def c1v(s):
            return ar_f32(22528 + s * 2048, 512)

        def c1_cells(s):
            return arcells(22528 + s * 2048, 22528 + (s + 1) * 2048)

        uc_all = ar_f32(0, 2048)

        def ucv(c):
            return uc_all[:, c * 512:(c + 1) * 512]

        def uc_cells(c):
            return arcells(c * 2048, (c + 1) * 2048)

        def ucbv(c):
            return ar_bf(8192 + c * 1024, 512)

        def ucb_cells(c):
            return arcells(8192 + c * 1024, 8192 + (c + 1) * 1024)

        def usqv(c):
            return ar_bf(12288 + c * 1024, 512)

        def usq_cells(c):
            return arcells(12288 + c * 1024, 12288 + (c + 1) * 1024)

        mean_sb = ar_f32(16384, 512)
        mean_cells = arcells(16384, 18432)
        var_sb = ar_f32(18432, 512)
        var_cells = arcells(18432, 20480)
        mr_sb = ar_f32(20480, 512)
        mr_cells = arcells(20480, 22528)

        def tlnv(s):
            return c1v(s)

        def tln_cells(s):
            return c1_cells(s)

        def stgv(s):
            return ar_f32(s * 4096, 1024)

        def stg_cells(s):
            return arcells(s * 4096, (s + 1) * 4096)

        def obv(s):
            return ar_bf(8192 + s * 2048, 1024)

        def ob_cells(s):
            return arcells(8192 + s * 2048, 8192 + (s + 1) * 2048)

        gkm = ar_f32(12288, 1024)
        gkm_cells = arcells(12288, 16384)
        gkf = ar_f32(16384, 1024)
        gkf_cells = arcells(16384, 20480)

        st_cbw = ar_f32(0, 512)
        st_caw = ar_f32(2048, 512)
        st_cfw = ar_f32(4096, DFF)
        st_vec = ar_f32(15360, 640)
        ST_CELLS = arcells(0, 17920)

        def par_loads(e):
            ins = []
            ins.append(e.dma_start(out=st_cbw[0:31, :], in_=conv_b_w))
            ins.append(e.dma_start(out=st_caw[0:3, :], in_=conv_a_w))
            ins.append(e.dma_start(out=st_cfw[0:3, :], in_=conv_ffn_w))
            ins.append(e.dma_start(out=st_vec[0:8, 0:128], in_=norm_mix_g.rearrange("(k p) -> k p", p=128)))
            ins.append(e.dma_start(out=st_vec[0:8, 128:256], in_=norm_ffn_g.rearrange("(k p) -> k p", p=128)))
            ins.append(e.dma_start(out=st_vec[0:4, 256:384], in_=conv_b_b.rearrange("(c p) -> c p", p=128)))
            ins.append(e.dma_start(out=st_vec[0:4, 384:512], in_=ln_b_g.rearrange("(c p) -> c p", p=128)))
            ins.append(e.dma_start(out=st_vec[0:4, 512:640], in_=ln_b_b.rearrange("(c p) -> c p", p=128)))
            ins.append(e.dma_start(out=gfin[:], in_=norm_final_g.partition_broadcast(128)))
            return ins
        P.dma("sp", par_loads, 9, "d_par", writes=ST_CELLS + ["gfin"])

        P.op("pool", lambda e: e.memset(identf[:], 0.0), writes=["identf"])
        P.op("pool", lambda e: e.affine_select(out=identf[:], in_=identf[:], pattern=[[-1, 128]],
                                               compare_op=ALU.not_equal, fill=1.0, base=0, channel_multiplier=1),
             reads=["identf"], writes=["identf"])
        P.op("dve", lambda e: e.tensor_copy(out=identb[:], in_=identf[:]), reads=["identf"], writes=["identb"])
        P.op("dve", lambda e: e.tensor_scalar(out=identh[:], in0=identf[:], scalar1=0.5, scalar2=None, op0=ALU.mult),
             reads=["identf"], writes=["identh"])

        def par_transposes():
            jobs = [
                (0, [(st_cbw[0:31, c * 128:(c + 1) * 128], 31) for c in range(4)], cbw[:, :, :].rearrange("p c t -> p (c t)"), "cbw"),
                (1, [(st_caw[0:3, c * 128:(c + 1) * 128], 3) for c in range(4)], caw[:, :, :].rearrange("p c t -> p (c t)"), "caw"),
                (2, [(st_cfw[0:3, j * 128:(j + 1) * 128], 3) for j in range(NF)], cfw[:, :, :].rearrange("p j t -> p (j t)"), "cfw"),
                (3, [(st_vec[0:8, 0:128], 8)], gmix[:, :], "gmix"),
                (0, [(st_vec[0:8, 128:256], 8)], gffn[:, :], "gffn"),
                (1, [(st_vec[0:4, 256:384], 4)], cbb[:, :], "cbb"),
                (2, [(st_vec[0:4, 384:512], 4)], lng[:, :], "lng"),
                (3, [(st_vec[0:4, 512:640], 4)], lnb[:, :], "lnb"),
            ]
            for bank, items, dst, name in jobs:
                def tr(e, bank=bank, items=items):
                    last = None
                    off = 0
                    for src_ap, r in items:
                        last = e.transpose(out=pb[bank][:, off:off + r], in_=src_ap, identity=identf[0:r, 0:r])
                        off += r
                    return last
                tot = sum(r for _, r in items)
                P.op("pe", tr, reads=ST_CELLS + ["identf"], writes=[PS(bank)])
                P.op("dve", (lambda e, bank=bank, dst=dst, tot=tot: e.tensor_copy(out=dst, in_=pb[bank][:, 0:tot])),
                     reads=[PS(bank)], writes=[name])

        def mk_consts(e):
            e.memset(onesb[:], 1.0 / 512.0)
            e.memset(epsc[:, 0:1], RMS_EPS)
            return e.memset(epsc[:, 1:2], LN_EPS)
        P.op("pool", mk_consts, writes=["onesb", "epsc"])

        def mk_pads(e):
            e.memset(upad[:], 0.0)
            return e.memset(pp[:], 0.0)
        P.op("pool", mk_pads, writes=[("upad", c) for c in range(4)] + [("pp", s) for s in range(2)])
        par_transposes()

        def load_wout():
            P.dma("pool", lambda e: [e.dma_start(out=wout[:], in_=w_out.rearrange("(c p) d -> p c d", p=128))], 1,
                  "d_wout", writes=["wout"])

        def load_wdn_part(q):
            j0 = 6 * q
            j1 = min(NF, j0 + 6)
            P.dma("pool", (lambda e: [e.dma_start(out=wdn[:, j0:j1, :],
                                                  in_=w_down[j0 * 128:j1 * 128, :].rearrange("(j p) d -> p j d", p=128))]),
                  1, "d_wdn%d" % q, writes=[("wdn", q)])
        WDN_CELLS = [("wdn", q) for q in range(4)]

        win_seq = [(ti, e_) for ti in range(len(tiles)) for e_ in A_ORDER]
        win_state = {"next": 0}

        def prefetch_win():
            n = win_state["next"]
            if n >= len(win_seq):
                return
            win_state["next"] = n + 1
            ti_, e_ = win_seq[n]
            s = n % N_WIN_SLOTS
            if ti_ == 0:
                P.dma("pool", (lambda e, s=s, e_=e_: [e.dma_start(
                    out=winr[:, s, :].rearrange("p (k c) -> p k c", k=KC),
                    in_=w_in[:, e_ * 128:(e_ + 1) * 128].rearrange("(k p) c -> p k c", p=128))]),
                    1, "d_winsw%d" % s, writes=[("winr", s)])
                if len(tiles) > 1:
                    P.dma("sp", (lambda e, s=s, e_=e_: [e.dma_start(out=scr_in[e_], in_=winr[:, s, :])]), 1,
                          "d_wst%d" % s, reads=[("winr", s)], writes=[("scr_in", e_)])
            else:
                P.dma("sp", (lambda e, s=s, e_=e_: [e.dma_start(out=winr[:, s, :], in_=scr_in[e_])]), 1, "d_win%d" % s,
                      reads=[("scr_in", e_)], writes=[("winr", s)])

        gu_total = len(tiles) * NF
        gu_state = {"next": 0}

        def prefetch_gu():
            n = gu_state["next"]
            if n >= gu_total:
                return
            gu_state["next"] = n + 1
            j = n % NF
            s = n % N_GU_SLOTS
            if n < NF:
                def ld(e, s=s, j=j):
                    return [e.dma_start(out=gur[:, s, w_, :].rearrange("p (k c) -> p k c", k=KC),
                                        in_=wsrc[:, j * 128:(j + 1) * 128].rearrange("(k p) c -> p k c", p=128))
                            for w_, wsrc in ((0, w_gate), (1, w_up))]
                P.dma("pool", ld, 2, "d_gusw%d" % s, writes=[("gur", s)])
                if len(tiles) > 1:
                    P.dma("sp", (lambda e, s=s, j=j: [e.dma_start(out=scr_gu[j], in_=gur[:, s, :, :])]), 1,
                          "d_gst%d" % s, reads=[("gur", s)], writes=[("scr_gu", j)])
            else:
                P.dma("sp", (lambda e, s=s, j=j: [e.dma_start(out=gur[:, s, :, :], in_=scr_gu[j])]), 1, "d_gu%d" % s,
                      reads=[("scr_gu", j)], writes=[("gur", s)])


        ctr = {"xa": 0, "stat": 0, "tp": 0, "ev": 0, "diag": 0, "hb": 0, "win": 0, "gu": 0, "c1": 0, "tln": 0}

        def nxt(name, mod):
            v = ctr[name]
            ctr[name] = v + 1
            return v % mod

        def rms_chain(src_ap, src_cells, m, junk_ap, junk_cells, eps_col):
            sc = nxt("stat", NSTAT)
            cell = ("stat", sc)
            P.op("act", (lambda e: e.activation(out=junk_ap, in_=src_ap, func=AF.Square, scale=1.0 / 32.0,
                                                accum_out=stat[0:m, sc, 0:1])),
                 reads=src_cells, writes=list(junk_cells) + [cell])
            P.op("act", (lambda e: e.activation(out=stat[0:m, sc, 1:2], in_=stat[0:m, sc, 0:1], func=AF.Sqrt,
                                                bias=epsc[0:m, eps_col:eps_col + 1], scale=1.0)),
                 reads=[cell, "epsc"], writes=[cell])
            P.op("dve", (lambda e: e.reciprocal(out=stat[0:m, sc, 2:3], in_=stat[0:m, sc, 1:2])),
                 reads=[cell], writes=[cell])
            return stat[0:m, sc, 2:3], cell

        def norm_a(src_ap, src_cells, m, buf_ap, buf_cell):
            rs, rcell = rms_chain(src_ap, src_cells, m, buf_ap, [buf_cell], 0)
            P.op("act", (lambda e: e.activation(out=buf_ap, in_=src_ap, func=AF.Copy, scale=rs)),
                 reads=list(src_cells) + [rcell], writes=[buf_cell])

        def norm_b(buf3, buf_cell, m, dstT, dst_cells, col0, gcol, gname):
            tb = nxt("tp", 2)

            def tr(e):
                last = None
                for k in range(KC):
                    last = e.transpose(out=tp[tb][:, k, 0:m], in_=buf3[:, k, :], identity=identb[0:m, 0:m])
                return last
            P.op("pe", tr, reads=[buf_cell, "identb"], writes=[PS(6 + tb)])
            P.op("dve", (lambda e: e.tensor_tensor(out=dstT[:, :, col0:col0 + m], in0=tp[tb][:, :, 0:m],
                                                   in1=gcol[:, 0:KC].unsqueeze(2).broadcast_to([128, KC, m]),
                                                   op=ALU.mult)),
                 reads=[PS(6 + tb), gname], writes=dst_cells)

        def s0_geom(ti):
            t0, T = tiles[ti]
            a0 = max(0, t0 - HALO)
            a1 = min(S, t0 + T + HALO)
            nA = a1 - a0
            return a0, nA, (nA + 127) // 128

        def s0_load(ti, c):
            a0, nA, nch = s0_geom(ti)
            if c >= nch:
                return
            m = min(128, nA - 128 * c)
            s = c % 2
            r0 = a0 + 128 * c
            P.dma("sp", (lambda e, s=s, r0=r0, m=m: [e.dma_start(out=xa[0:m, s, :], in_=x[r0:r0 + m, :])]), 1,
                  "d_xa%d" % s, writes=[("xa", s)])

        def s0_norm(ti, c):
            a0, nA, nch = s0_geom(ti)
            if c >= nch:
                return
            m = min(128, nA - 128 * c)
            s = c % 2
            norm_a(xa[0:m, s, :], [("xa", s)], m, hbs[0:m, c, :], ("hbs", c))

        def stage_S0b(ti, c):
            a0, nA, nch = s0_geom(ti)
            if c >= nch:
                return
            m = min(128, nA - 128 * c)
            norm_b(hbs[0:m, c, :].rearrange("p (k c) -> p k c", k=KC), ("hbs", c), m, hT, [("hT", c)], 128 * c,
                   gmix, "gmix")

        def stage_mixer(ti):
            t0, T = tiles[ti]
            a0 = max(0, t0 - HALO)
            a1 = min(S, t0 + T + HALO)
            nA = a1 - a0
            b0 = max(0, t0 - 1)
            b1 = min(S, t0 + T + 1)
            nB = b1 - b0
            oB = b0 - a0
            ncA = (nA + 127) // 128
            ntc = (nB + 127) // 128

            for tc in range(ntc):
                m = min(128, nB - 128 * tc)
                r0 = b0 + 128 * tc
                P.dma("pool", (lambda e, tc=tc, r0=r0, m=m: [e.dma_start(out=x1[0:m, tc, :], in_=x[r0:r0 + m, :])]), 1,
                      "d_x1_%d" % tc, writes=[("x1", tc)])

            hT_cells = [("hT", c) for c in range(ncA)]
            abank = {"n": 0}
            evslot = {}
            ctslot = {}
            ppslot = {}

            if nA < WIN:
                P.op("pool", (lambda e: e.memset(upad[:, :, 15 + nA:UPW], 0.0)),
                     writes=[("upad", c) for c in range(4)])

            def a_group(e_):
                ws = nxt("win", N_WIN_SLOTS)
                bank = abank["n"] % 4
                abank["n"] += 1

                if e_ < 12:
                    q0, q1 = max(0, oB - 1), min(nA, oB + nB + 1)
                else:
                    q0, q1 = 0, nA

                def mmA(e, ws=ws, bank=bank, q0=q0, q1=q1):
                    last = None
                    for k in range(KC):
                        last = e.matmul(pb[bank][:, q0:q1], lhsT=winr[:, ws, k * 128:(k + 1) * 128], rhs=hT[:, k, q0:q1],
                                        start=(k == 0), stop=(k == KC - 1))
                    return last
                P.op("pe", mmA, reads=[("winr", ws)] + hT_cells, writes=[PS(bank)])
                prefetch_win()
                if ti == 0 and e_ == A_ORDER[-5]:
                    for _ in range(N_GU_SLOTS):
                        prefetch_gu()
                    load_wout()
                    for q in range(4):
                        load_wdn_part(q)
                kind, c = e_ // 4, e_ % 4
                if kind == 4:
                    s = nxt("ev", 3)
                    evslot[("g", c)] = s
                    P.op("act", (lambda e, s=s, bank=bank: e.activation(out=ev[:, s, 0:nA], in_=pb[bank][:, 0:nA],
                                                                         func=AF.Tanh, scale=0.5)),
                         reads=[PS(bank)], writes=[("ev", s)])
                elif kind == 3:
                    s = evslot[("g", c)]
                    P.op("dve", (lambda e, s=s, bank=bank, c=c: e.scalar_tensor_tensor(
                        out=upad[:, c, 15:15 + nA], in0=ev[:, s, 0:nA], scalar=1.0, in1=pb[bank][:, 0:nA],
                        op0=ALU.add, op1=ALU.mult)),
                        reads=[("ev", s), PS(bank)], writes=[("upad", c)])
                elif kind == 0:
                    s = nxt("ev", 3)
                    evslot[("h", c)] = s
                    P.op("act", (lambda e, s=s, bank=bank, q0=q0, q1=q1: e.activation(
                        out=ev[:, s, q0:q1], in_=pb[bank][:, q0:q1], func=AF.Copy)),
                         reads=[PS(bank)], writes=[("ev", s)])
                elif kind == 2:
                    s = evslot[("h", c)]
                    ps_ = c % 2
                    cs = c
                    ppslot[c] = ps_
                    ctslot[c] = cs
                    P.op("dve", (lambda e, s=s, bank=bank, ps_=ps_, q0=q0, q1=q1: e.tensor_tensor(
                        out=pp[:, ps_, 1 + q0:1 + q1], in0=pb[bank][:, q0:q1], in1=ev[:, s, q0:q1], op=ALU.mult)),
                        reads=[("ev", s), PS(bank)], writes=[("pp", ps_)])
                    if nA < WIN:
                        P.op("pool", (lambda e, ps_=ps_: e.memset(pp[:, ps_, 1 + nA:PPW], 0.0)), writes=[("pp", ps_)])
                    P.op("act", (lambda e, ps_=ps_, cs=cs, c=c: e.activation(
                        out=ct[:, cs, 0:nB], in_=pp[:, ps_, 1 + oB:1 + oB + nB], func=AF.Copy, scale=caw[:, c, 1:2])),
                        reads=[("pp", ps_), "caw"], writes=[("ct", cs)])
                    P.op("dve", (lambda e, ps_=ps_, cs=cs, c=c: e.scalar_tensor_tensor(
                        out=ct[:, cs, 0:nB], in0=pp[:, ps_, oB:oB + nB], scalar=caw[:, c, 0:1], in1=ct[:, cs, 0:nB],
                        op0=ALU.mult, op1=ALU.add)), reads=[("pp", ps_), ("ct", cs), "caw"], writes=[("ct", cs)])
                    P.op("dve", (lambda e, ps_=ps_, cs=cs, c=c: e.scalar_tensor_tensor(
                        out=ct[:, cs, 0:nB], in0=pp[:, ps_, 2 + oB:2 + oB + nB], scalar=caw[:, c, 2:3],
                        in1=ct[:, cs, 0:nB], op0=ALU.mult, op1=ALU.add)),
                        reads=[("pp", ps_), ("ct", cs), "caw"], writes=[("ct", cs)])
                elif kind == 1:
                    cs = ctslot[c]
                    P.op("dve", (lambda e, cs=cs, bank=bank, c=c: e.tensor_tensor(
                        out=yT[:, c, 0:nB], in0=pb[bank][:, oB:oB + nB], in1=ct[:, cs, 0:nB], op=ALU.mult)),
                        reads=[("ct", cs), PS(bank)], writes=[("yT", c)])

            diag_q = []

            def gen_diag(idx):
                c, kb = idx // 4, idx % 4
                k0 = 8 * kb
                nt = min(8, 31 - k0)
                ds = nxt("diag", N_DIAG_BLK)
                P.op(DIAG_ENGINE, (lambda e, ds=ds, c=c, k0=k0, nt=nt: e.tensor_tensor(
                    out=diag[:, ds, 0:nt, :], in0=identh[:].unsqueeze(1).broadcast_to([128, nt, 128]),
                    in1=cbw[:, c, k0:k0 + nt].unsqueeze(2).broadcast_to([128, nt, 128]), op=ALU.mult)),
                    reads=["identh", "cbw"], writes=[("diag", ds)])
                diag_q.append(ds)

            diag_state = {"gen": 0}

            def ensure_diag(upto):
                while diag_state["gen"] <= upto and diag_state["gen"] < 16:
                    gen_diag(diag_state["gen"])
                    diag_state["gen"] += 1

            def conv_chunk(c):
                bank = 4 + (c % 2)
                for kb in range(4):
                    k0 = 8 * kb
                    nt = min(8, 31 - k0)
                    idx = 4 * c + kb
                    ensure_diag(idx)
                    ds = diag_q[idx]

                    def mmC(e, ds=ds, c=c, k0=k0, nt=nt, bank=bank):
                        last = None
                        for t in range(nt):
                            k = k0 + t
                            last = e.matmul(pb[bank][:, 0:nB], lhsT=diag[:, ds, t, :],
                                            rhs=upad[:, c, oB + k:oB + k + nB], start=(k == 0), stop=(k == 30))
                        return last
                    P.op("pe", mmC, reads=[("diag", ds), ("upad", c)], writes=[PS(bank)])
                    ensure_diag(min(15, idx + N_DIAG_BLK - 1))
                P.op("act", (lambda e, c=c, bank=bank: e.activation(out=ucbv(c)[:, 0:nB], in_=pb[bank][:, 0:nB],
                                                                     func=AF.Identity, bias=cbb[:, c:c + 1], scale=1.0)),
                     reads=[PS(bank), "cbb"], writes=ucb_cells(c))
                P.op("act", (lambda e, c=c, bank=bank: e.activation(out=usqv(c)[:, 0:nB], in_=pb[bank][:, 0:nB],
                                                                     func=AF.Square, bias=cbb[:, c:c + 1], scale=1.0)),
                     reads=[PS(bank), "cbb"], writes=usq_cells(c))
                P.op("act", (lambda e, c=c, bank=bank: e.activation(out=ucv(c)[:, 0:nB], in_=pb[bank][:, 0:nB],
                                                                     func=AF.Identity, bias=cbb[:, c:c + 1], scale=1.0)),
                     reads=[PS(bank), "cbb"], writes=uc_cells(c))

            def stat_chunk(c):
                P.op("pe", (lambda e, c=c: e.matmul(stat_ps[0][:, 0:nB], lhsT=onesb[:], rhs=ucbv(c)[:, 0:nB],
                                                     start=(c == 0), stop=(c == 3))),
                     reads=["onesb"] + ucb_cells(c), writes=[PS(6)])
                P.op("pe", (lambda e, c=c: e.matmul(stat_ps[1][:, 0:nB], lhsT=onesb[:], rhs=usqv(c)[:, 0:nB],
                                                     start=(c == 0), stop=(c == 3))),
                     reads=["onesb"] + usq_cells(c), writes=[PS(7)])

            ensure_diag(N_DIAG_BLK - 2)
            for e_ in A_ORDER[:8]:
                a_group(e_)
            conv_chunk(0)
            conv_chunk(1)
            stat_chunk(0)
            conv_chunk(2)
            stat_chunk(1)
            conv_chunk(3)
            stat_chunk(2)
            stat_chunk(3)

            L = []
            L.append(lambda: P.op("act", (lambda e: e.activation(out=mean_sb[:, 0:nB], in_=stat_ps[0][:, 0:nB],
                                                                 func=AF.Copy)), reads=[PS(6)], writes=mean_cells))
            L.append(lambda: P.op("act", (lambda e: e.activation(out=var_sb[:, 0:nB], in_=stat_ps[0][:, 0:nB],
                                                                 func=AF.Square)), reads=[PS(6)], writes=var_cells))
            L.append(lambda: P.op("dve", (lambda e: e.tensor_tensor(out=var_sb[:, 0:nB], in0=stat_ps[1][:, 0:nB],
                                                                    in1=var_sb[:, 0:nB], op=ALU.subtract)),
                                  reads=[PS(7)] + var_cells, writes=var_cells))
            L.append(lambda: P.op("act", (lambda e: e.activation(out=var_sb[:, 0:nB], in_=var_sb[:, 0:nB], func=AF.Sqrt,
                                                                 bias=epsc[:, 1:2], scale=1.0)),
                                  reads=var_cells + ["epsc"], writes=var_cells))
            L.append(lambda: P.op("dve", (lambda e: e.reciprocal(out=var_sb[:, 0:nB], in_=var_sb[:, 0:nB])),
                                  reads=var_cells, writes=var_cells))
            LN_mul, LN_sub, LN_silu = [], [], []
            for c in range(4):
                eng_c = "dve" if c % 2 == 0 else LN_ENGINE
                LN_sub.append(lambda c=c, eng_c=eng_c: P.op(eng_c, (lambda e: e.tensor_tensor(
                    out=ucv(c)[:, 0:nB], in0=ucv(c)[:, 0:nB], in1=mean_sb[:, 0:nB], op=ALU.subtract)),
                    reads=uc_cells(c) + mean_cells, writes=uc_cells(c)))
                LN_mul.append(lambda c=c, eng_c=eng_c: P.op(eng_c, (lambda e: e.tensor_tensor(
                    out=ucv(c)[:, 0:nB], in0=ucv(c)[:, 0:nB], in1=var_sb[:, 0:nB], op=ALU.mult)),
                    reads=uc_cells(c) + var_cells, writes=uc_cells(c)))
                LN_silu.append(lambda c=c: P.op("act", (lambda e: e.activation(
                    out=yT[:, 4 + c, 0:nB], in_=ucv(c)[:, 0:nB], func=AF.Silu, scale=lng[:, c:c + 1],
                    bias=lnb[:, c:c + 1])), reads=uc_cells(c) + ["lng", "lnb"], writes=[("yT", 4 + c)]))
            sched = {
                0: [L[0], L[1]],
                1: [L[2], LN_sub[0], LN_sub[1]],
                2: [L[3], LN_sub[2], LN_sub[3]],
                4: [L[4]],
                5: [LN_mul[0], LN_mul[1]],
                6: [LN_mul[2], LN_mul[3]],
                8: [LN_silu[0], LN_silu[1]],
                9: [LN_silu[2], LN_silu[3]],
            }
            for gi, e_ in enumerate(A_ORDER[8:]):
                a_group(e_)
                for th in sched.get(gi, []):
                    th()

            yT_cells = [("yT", c) for c in range(8)]
            obank = {"n": 0}
            pend_b = None
            for tc in range(ntc):
                m = min(128, nB - 128 * tc)
                for dh in range(2):
                    bank = 4 + (obank["n"] % 2)
                    obank["n"] += 1

                    def mmO(e, tc=tc, m=m, dh=dh, bank=bank):
                        last = None
                        for cc in range(KC):
                            last = e.matmul(pb[bank][0:m, :], lhsT=yT[:, cc, 128 * tc:128 * tc + m],
                                            rhs=wout[:, cc, dh * 512:(dh + 1) * 512], start=(cc == 0), stop=(cc == KC - 1))
                        return last
                    P.op("pe", mmO, reads=yT_cells + ["wout"], writes=[PS(bank)])
                    P.op("dve", (lambda e, tc=tc, m=m, dh=dh, bank=bank: e.tensor_tensor(
                        out=x1[0:m, tc, dh * 512:(dh + 1) * 512], in0=pb[bank][0:m, :],
                        in1=x1[0:m, tc, dh * 512:(dh + 1) * 512], op=ALU.add)),
                        reads=[PS(bank), ("x1", tc)], writes=[("x1", tc)])
                hs = nxt("hb", 3)
                norm_a(x1[0:m, tc, :], [("x1", tc)], m, hbf[0:m, hs, :], ("hbf", hs))
                if pend_b is not None:
                    pend_b()

                def pb_(tc=tc, m=m, hs=hs):
                    norm_b(hbf[0:m, hs, :].rearrange("p (k c) -> p k c", k=KC), ("hbf", hs), m, h2T, [("h2T", tc)],
                           128 * tc, gffn, "gffn")
                pend_b = pb_
            return nB, ntc, b0, pend_b

        def stage_ffn(ti, nB, ntc, b0, hoist, pend_b):
            t0, T = tiles[ti]
            h2T_cells = [("h2T", tc) for tc in range(ntc)]
            pending_mul = None
            tails = []
            evq = []
            if ntc <= 1:
                pend_b()
            for j in range(NF):
                gs = nxt("gu", N_GU_SLOTS)
                if j < 18:
                    gb = (0, 1, 2)[j % 3]
                    vb = (4, 5, 3)[j % 3]
                else:
                    gb = (0, 1, 6)[j % 3]
                    vb = (4, 5, 7)[j % 3]

                def mk_mm(w_, bank, c0, c1, gs=gs):
                    def mm(e):
                        last = None
                        for k in range(KC):
                            last = e.matmul(pbv(bank)[:, c0:c1], lhsT=gur[:, gs, w_, k * 128:(k + 1) * 128],
                                            rhs=h2T[:, k, c0:c1], start=(k == 0), stop=(k == KC - 1))
                        return last
                    return mm
                if j < 2 and ntc > 1:
                    c_head = 128 * max(1, ntc - 2)
                    pieces = [(0, c_head)]
                    if ntc > 2:
                        pieces.append((c_head, 128 * (ntc - 1)))
                    pieces.append((128 * (ntc - 1), nB))
                    hd = h2T_cells[:max(1, ntc - 2)]
                    P.op("pe", mk_mm(0, gb, 0, c_head), reads=[("gur", gs)] + hd, writes=[PS(gb)])
                    P.op("pe", mk_mm(1, vb, 0, c_head), reads=[("gur", gs)] + hd, writes=[PS(vb)])
                    tails.append((gs, gb, vb))
                    if j == 1:
                        pend_b()
                        for pi_, (c0_, c1_) in enumerate(pieces[1:]):
                            need = h2T_cells[:(c1_ + 127) // 128]
                            for (gs_, gb_, vb_) in tails:
                                P.op("pe", mk_mm(0, gb_, c0_, c1_, gs_), reads=[("gur", gs_)] + need, writes=[PS(gb_)])
                                P.op("pe", mk_mm(1, vb_, c0_, c1_, gs_), reads=[("gur", gs_)] + need, writes=[PS(vb_)])
                else:
                    P.op("pe", mk_mm(0, gb, 0, nB), reads=[("gur", gs)] + h2T_cells, writes=[PS(gb)])
                    P.op("pe", mk_mm(1, vb, 0, nB), reads=[("gur", gs)] + h2T_cells, writes=[PS(vb)])
                if j < 2 and ntc > 1:
                    if j == 1:
                        prefetch_gu()
                        prefetch_gu()
                else:
                    prefetch_gu()
                def evac(j=j, gb=gb, vb=vb):
                    nonlocal pending_mul
                    cs = nxt("c1", 2)
                    cc_ = c1_cells(cs)
                    P.op("act", (lambda e, cs=cs, gb=gb, j=j: e.activation(out=c1v(cs)[:, 0:nB], in_=pbv(gb)[:, 0:nB],
                                                                            func=AF.Copy, scale=cfw[:, j, 1:2])),
                         reads=[PS(gb), "cfw"], writes=cc_)
                    P.op("dve", (lambda e, cs=cs, gb=gb, j=j: e.scalar_tensor_tensor(
                        out=c1v(cs)[:, 1:nB], in0=pbv(gb)[:, 0:nB - 1], scalar=cfw[:, j, 0:1], in1=c1v(cs)[:, 1:nB],
                        op0=ALU.mult, op1=ALU.add)), reads=[PS(gb), "cfw"] + cc_, writes=cc_)
                    if pending_mul is not None:
                        pending_mul()
                    P.op("dve", (lambda e, cs=cs, gb=gb, j=j: e.scalar_tensor_tensor(
                        out=c1v(cs)[:, 0:nB - 1], in0=pbv(gb)[:, 1:nB], scalar=cfw[:, j, 2:3], in1=c1v(cs)[:, 0:nB - 1],
                        op0=ALU.mult, op1=ALU.add)), reads=[PS(gb), "cfw"] + cc_, writes=cc_)
                    P.op("act", (lambda e, cs=cs: e.activation(out=c1v(cs)[:, 0:nB], in_=c1v(cs)[:, 0:nB], func=AF.Silu)),
                         reads=cc_, writes=cc_)

                    def mul(cs=cs, vb=vb, j=j, cc_=cc_):
                        P.op("dve", (lambda e: e.tensor_tensor(out=aT(j)[:, 0:nB], in0=pbv(vb)[:, 0:nB],
                                                               in1=c1v(cs)[:, 0:nB], op=ALU.mult)),
                             reads=[PS(vb)] + cc_, writes=aT_cells(j))
                    pending_mul = mul
                if j < 2 and ntc > 1:
                    evq.append(evac)
                    if j == 1:
                        for ev_ in evq:
                            ev_()
                else:
                    evac()
            pending_mul()

            aT_all = [c for j in range(NF) for c in aT_cells(j)]
            dbank = {"n": 0}
            if hoist is not None:
                s0_load(hoist, 0)
                s0_load(hoist, 1)
            for tc in range(ntc):
                m = min(128, nB - 128 * tc)
                for dh in range(2):
                    bank = 2 + (dbank["n"] % 2)
                    dbank["n"] += 1

                    def mmD(e, tc=tc, m=m, dh=dh, bank=bank):
                        last = None
                        for j in range(NF):
                            last = e.matmul(pb[bank][0:m, :], lhsT=aT(j)[:, 128 * tc:128 * tc + m],
                                            rhs=wdn[:, j, dh * 512:(dh + 1) * 512], start=(j == 0), stop=(j == NF - 1))
                        return last
                    if dbank["n"] == 1:
                        def mmD1(e, tc=tc, m=m, dh=dh, bank=bank):
                            last = None
                            for j in range(NF - 4):
                                last = e.matmul(pb[bank][0:m, :], lhsT=aT(j)[:, 128 * tc:128 * tc + m],
                                                rhs=wdn[:, j, dh * 512:(dh + 1) * 512], start=(j == 0), stop=False)
                            return last

                        def mmD2(e, tc=tc, m=m, dh=dh, bank=bank):
                            last = None
                            for j in range(NF - 4, NF):
                                last = e.matmul(pb[bank][0:m, :], lhsT=aT(j)[:, 128 * tc:128 * tc + m],
                                                rhs=wdn[:, j, dh * 512:(dh + 1) * 512], start=False, stop=(j == NF - 1))
                            return last
                        head_cells = [c for j in range(NF - 4) for c in aT_cells(j)]
                        P.op("pe", mmD1, reads=head_cells + WDN_CELLS, writes=[PS(bank)])
                        P.op("pe", mmD2, reads=aT_all + WDN_CELLS, writes=[PS(bank)])
                    else:
                        P.op("pe", mmD, reads=aT_all + WDN_CELLS, writes=[PS(bank)])
                    if hoist is not None:
                        g_ = dbank["n"] - 1
                        if g_ == 0:
                            s0_norm(hoist, 0)
                            s0_load(hoist, 2)
                        elif g_ == 1:
                            s0_norm(hoist, 1)
                            s0_load(hoist, 3)
                        elif g_ == 2:
                            stage_S0b(hoist, 0)
                            s0_norm(hoist, 2)
                        elif g_ == 3:
                            stage_S0b(hoist, 1)
                            s0_norm(hoist, 3)
                        elif g_ == 4:
                            stage_S0b(hoist, 2)
                        elif g_ == 5:
                            stage_S0b(hoist, 3)
                    P.op("dve", (lambda e, tc=tc, m=m, dh=dh, bank=bank: e.tensor_tensor(
                        out=x1[0:m, tc, dh * 512:(dh + 1) * 512], in0=pb[bank][0:m, :],
                        in1=x1[0:m, tc, dh * 512:(dh + 1) * 512], op=ALU.add)),
                        reads=[PS(bank), ("x1", tc)], writes=[("x1", tc)])
                hs = nxt("hb", 3)
                rs, rcell = rms_chain(x1[0:m, tc, :], [("x1", tc)], m, hbf[0:m, hs, :], [("hbf", hs)], 0)
                P.op("dve", (lambda e, tc=tc, m=m, rs=rs: e.scalar_tensor_tensor(
                    out=x1[0:m, tc, :], in0=x1[0:m, tc, :], scalar=rs, in1=gfin[0:m, :], op0=ALU.mult, op1=ALU.mult)),
                    reads=[("x1", tc), rcell, "gfin"], writes=[("x1", tc)])
                r_lo = max(0, (t0 - b0) - 128 * tc)
                r_hi = min(m, (t0 + T - b0) - 128 * tc)
                if r_lo < r_hi:
                    g0 = b0 + 128 * tc
                    tok = P.dma("pool", (lambda e, tc=tc, r_lo=r_lo, r_hi=r_hi, g0=g0: [e.dma_start(
                        out=y[g0 + r_lo:g0 + r_hi, :], in_=x1[r_lo:r_hi, tc, :])]), 1, "d_x1_%d" % tc,
                        reads=[("x1", tc)])
                    P.final_tokens.append(tok)

        s0_load(0, 0)
        s0_load(0, 1)
        s0_norm(0, 0)
        s0_load(0, 2)
        s0_norm(0, 1)
        s0_load(0, 3)
        for _ in range(N_WIN_SLOTS):
            prefetch_win()
        stage_S0b(0, 0)
        stage_S0b(0, 1)
        s0_norm(0, 2)
        stage_S0b(0, 2)
        s0_norm(0, 3)
        stage_S0b(0, 3)
        for ti in range(len(tiles)):
            nB, ntc, b0, pend_b = stage_mixer(ti)
            hoist = ti + 1 if ti + 1 < len(tiles) else None
            stage_ffn(ti, nB, ntc, b0, hoist, pend_b)

        P.emit(nc, es)
    return nc


_PARAM_KEYS = ["norm_mix_g", "w_in", "conv_a_w", "conv_b_w", "conv_b_b", "ln_b_g", "ln_b_b", "w_out",
               "norm_ffn_g", "w_gate", "w_up", "conv_ffn_w", "w_down", "norm_final_g"]


def _prep_params(inputs):
    out = {}
    for k in _PARAM_KEYS:
        a = np.asarray(inputs[k], dtype=np.float32)
        if k != "norm_final_g":
            a = a[0]
        out[k] = np.ascontiguousarray(a)
    return out


def kernel(**inputs):
    x = np.asarray(inputs["x"], dtype=np.float32)
    B, S, _ = x.shape
    params = _prep_params(inputs)
    nc = build(S)
    in_maps = []
    for b in range(B):
        m = dict(params)
        m["x"] = np.ascontiguousarray(x[b])
        in_maps.append(m)
    res = run_bass_kernel_spmd(nc, in_maps, core_ids=list(range(B)))
    return np.stack([np.asarray(r["y"], dtype=np.float32) for r in res.results], axis=0)
```

```python
import numpy as np
from contextlib import ExitStack

import concourse.bass as bass
import concourse.mybir as mybir
from concourse.bass_utils import run_bass_kernel_spmd

F32 = mybir.dt.float32
BF16 = mybir.dt.bfloat16
AF = mybir.ActivationFunctionType
ALU = mybir.AluOpType

D = 1024
DA = 512
DB = 512
DIN = 2560
DFF = 2816
NF = DFF // 128
KC = D // 128
NE = DIN // 128
RMS_EPS = 1e-6
LN_EPS = 1e-5
HALO = 16
WIN = 512

A_ORDER = [16, 12, 17, 13, 18, 14, 19, 15, 0, 8, 1, 9, 2, 10, 3, 11, 4, 5, 6, 7]

N_WIN_SLOTS = 5
N_GU_SLOTS = 3
N_DIAG_BLK = 3
DIAG_ENGINE = "dve"
LN_ENGINE = "pool"


def plan_tiles(S):
    tiles = []
    t0 = 0
    while t0 < S:
        if t0 == 0:
            T = min(S, WIN - HALO)
        else:
            T = min(WIN - 2 * HALO, S - t0)
            if S - t0 <= WIN - HALO:
                T = S - t0
        tiles.append((t0, T))
        t0 += T
    return tiles


class Prog:
    COMPUTE = ("act", "dve", "pool", "pe")

    def __init__(self):
        self.streams = {e: [] for e in ("sp", "act", "dve", "pool", "pe")}
        self.ccount = {e: 0 for e in self.COMPUTE}
        self.dcount = {}
        self.state = {}
        self.waited = {e: {} for e in self.streams}
        self.semnames = set("c_" + e for e in self.COMPUTE)
        self.final_tokens = []

    def _deps(self, eng, reads, writes, is_dma):
        need = {}

        def add(tok, raw):
            sem, val, src = tok
            if src == eng and not is_dma:
                if eng == "pe":
                    return
                if not raw:
                    return
            if need.get(sem, 0) < val:
                need[sem] = val

        for c in reads:
            st = self.state.get(c)
            if st is not None and st[0] is not None:
                add(st[0], True)
        for c in writes:
            st = self.state.get(c)
            if st is not None:
                if st[0] is not None:
                    add(st[0], False)
                for t in st[1].values():
                    add(t, False)
        waits = []
        wd = self.waited[eng]
        for sem, val in need.items():
            if wd.get(sem, 0) < val:
                wd[sem] = val
                waits.append((sem, val))
        return waits

    def _commit(self, tok, reads, writes):
        for c in reads:
            st = self.state.get(c)
            if st is None:
                st = [None, {}]
                self.state[c] = st
            st[1][(tok[0], tok[2])] = tok
        for c in writes:
            self.state[c] = [tok, {}]

    def op(self, eng, fn, reads=(), writes=()):
        reads = tuple(reads)
        writes = tuple(writes)
        waits = self._deps(eng, reads, writes, False)
        self.ccount[eng] += 1
        tok = ("c_" + eng, self.ccount[eng], eng)
        self.streams[eng].append((waits, fn, "c_" + eng, 1))
        self._commit(tok, reads, writes)
        return tok

    def dma(self, queue, fn, n, sem, reads=(), writes=()):
        reads = tuple(reads)
        writes = tuple(writes)
        waits = self._deps(queue, reads, writes, True)
        self.semnames.add(sem)
        self.dcount[sem] = self.dcount.get(sem, 0) + n
        tok = (sem, 16 * self.dcount[sem], queue)
        self.streams[queue].append((waits, fn, sem, 16))
        self._commit(tok, reads, writes)
        return tok

    def emit(self, nc, es):
        sems = {name: es.enter_context(nc.semaphore(name)) for name in sorted(self.semnames)}
        block = es.enter_context(nc.Block())
        streams = self.streams
        finals = self.final_tokens

        awaited = set()
        for name in streams:
            for waits, fn, sem, inc in streams[name]:
                for s, v in waits:
                    awaited.add((s, v))
        remap = {}
        for name in self.COMPUTE:
            sem = "c_" + name
            new = 0
            idx = 0
            m = {}
            for waits, fn, s_, inc in streams[name]:
                if inc != 1:
                    continue
                idx += 1
                if (sem, idx) in awaited:
                    new += 1
                    m[idx] = new
            remap[sem] = m

        def run(eng, name):
            idx = 0
            for waits, fn, sem, inc in streams[name]:
                for s, v in waits:
                    if s in remap:
                        v = remap[s][v]
                    eng.wait_ge(sems[s], v)
                res = fn(eng)
                if inc == 16:
                    for ins in res:
                        ins.then_inc(sems[sem], 16)
                else:
                    idx += 1
                    if idx in remap[sem]:
                        last = res[-1] if isinstance(res, (list, tuple)) else res
                        last.then_inc(sems[sem], 1)
            if name == "sp":
                done = {}
                for s, v, _ in finals:
                    done[s] = max(done.get(s, 0), v)
                for s, v in done.items():
                    eng.wait_ge(sems[s], v)

        @block.sync
        def _(e):
            run(e, "sp")

        @block.scalar
        def _(e):
            run(e, "act")

        @block.vector
        def _(e):
            run(e, "dve")

        @block.gpsimd
        def _(e):
            run(e, "pool")

        @block.tensor
        def _(e):
            run(e, "pe")


def arcells(lo, hi):
    return [("ar", i) for i in range(lo // 1024, (hi - 1) // 1024 + 1)]


def build(S):
    nc = bass.Bass("TRN2", target_bir_lowering=False)
    dt = nc.dram_tensor
    x = dt("x", [S, D], F32, kind="ExternalInput").ap()
    norm_mix_g = dt("norm_mix_g", [D], F32, kind="ExternalInput").ap()
    w_in = dt("w_in", [D, DIN], F32, kind="ExternalInput").ap()
    conv_a_w = dt("conv_a_w", [3, DA], F32, kind="ExternalInput").ap()
    conv_b_w = dt("conv_b_w", [31, DB], F32, kind="ExternalInput").ap()
    conv_b_b = dt("conv_b_b", [DB], F32, kind="ExternalInput").ap()
    ln_b_g = dt("ln_b_g", [DB], F32, kind="ExternalInput").ap()
    ln_b_b = dt("ln_b_b", [DB], F32, kind="ExternalInput").ap()
    w_out = dt("w_out", [D, D], F32, kind="ExternalInput").ap()
    norm_ffn_g = dt("norm_ffn_g", [D], F32, kind="ExternalInput").ap()
    w_gate = dt("w_gate", [D, DFF], F32, kind="ExternalInput").ap()
    w_up = dt("w_up", [D, DFF], F32, kind="ExternalInput").ap()
    conv_ffn_w = dt("conv_ffn_w", [3, DFF], F32, kind="ExternalInput").ap()
    w_down = dt("w_down", [DFF, D], F32, kind="ExternalInput").ap()
    norm_final_g = dt("norm_final_g", [D], F32, kind="ExternalInput").ap()
    y = dt("y", [S, D], F32, kind="ExternalOutput").ap()
    scr_in = dt("scr_in", [NE, 128, 1024], BF16, kind="Internal").ap()
    scr_gu = dt("scr_gu", [NF, 128, 2, 1024], BF16, kind="Internal").ap()

    tiles = plan_tiles(S)
    P = Prog()

    with ExitStack() as es:
        def sb(name, shape, dtype):
            return es.enter_context(nc.sbuf_tensor(name, shape, dtype))

        wout = sb("wout", [128, KC, D], BF16)
        wdn = sb("wdn", [128, NF, D], BF16)
        winr = sb("winr", [128, N_WIN_SLOTS, 1024], BF16)
        gur = sb("gur", [128, N_GU_SLOTS, 2, 1024], BF16)
        diag = sb("diag", [128, N_DIAG_BLK, 8, 128], BF16)
        gfin = sb("gfin", [128, D], F32)
        identf = sb("identf", [128, 128], F32)
        identh = sb("identh", [128, 128], F32)
        identb = sb("identb", [128, 128], BF16)
        onesb = sb("onesb", [128, 128], BF16)
        gmix = sb("gmix", [128, KC], F32)
        gffn = sb("gffn", [128, KC], F32)
        caw = sb("caw", [128, 4, 3], F32)
        cbw = sb("cbw", [128, 4, 31], F32)
        cbb = sb("cbb", [128, 4], F32)
        lng = sb("lng", [128, 4], F32)
        lnb = sb("lnb", [128, 4], F32)
        cfw = sb("cfw", [128, NF, 3], F32)
        epsc = sb("epsc", [128, 2], F32)
        xa = sb("xa", [128, 2, D], F32)
        hbf = sb("hbf", [128, 3, D], BF16)
        hbs = sb("hbs", [128, 4, D], BF16)
        hT = sb("hT", [128, KC, WIN], BF16)
        h2T = sb("h2T", [128, KC, WIN], BF16)
        yT = sb("yT", [128, KC, WIN], BF16)
        x1 = sb("x1", [128, 4, D], F32)
        NSTAT = 16
        stat = sb("stat", [128, NSTAT, 4], F32)
        ev = sb("ev", [128, 3, WIN], F32)
        UPW = 544
        upad = sb("upad", [128, 4, UPW], BF16)
        PPW = 516
        pp = sb("pp", [128, 2, PPW], F32)
        ct = sb("ct", [128, 4, WIN], F32)
        ARENA_E = 13312
        arena = sb("arena", [128, ARENA_E], BF16)

        warm = sb("warm", [128, 2], F32)

        def act_warm(func):
            P.op("act", (lambda e: e.activation(out=warm[:, 0:1], in_=epsc[:, 0:1], func=func)),
                 reads=["epsc"], writes=["warm"])

        pb = [es.enter_context(nc.psum_tensor("pb%d" % i, [128, 512], F32)) for i in range(6)]
        tp = [es.enter_context(nc.psum_tensor("tp%d" % i, [128, KC, 128], BF16)) for i in range(2)]

        def PS(b):
            return ("ps", b)

        stat_ps = [tp[b][:, :, :].rearrange("p k c -> p (k c)").bitcast(F32) for b in range(2)]

        def pbv(b):
            return pb[b] if b < 6 else stat_ps[b - 6]

        def ar_bf(lo_b, n):
            return arena[:, lo_b // 2: lo_b // 2 + n]

        def ar_f32(lo_b, n):
            return arena[:, lo_b // 2: lo_b // 2 + 2 * n].bitcast(F32)

        def aT(j):
            return ar_bf(j * 1024, 512)

        def aT_cells(j):
            return arcells(j * 1024, (j + 1) * 1024)

        def c1v(s):
            return ar_f32(22528 + s * 2048, 512)

        def c1_cells(s):
            return arcells(22528 + s * 2048, 22528 + (s + 1) * 2048)

        uc_all = ar_f32(0, 2048)

        def ucv(c):
            return uc_all[:, c * 512:(c + 1) * 512]

        def uc_cells(c):
            return arcells(c * 2048, (c + 1) * 2048)

        def ucbv(c):
            return ar_bf(8192 + c * 1024, 512)

        def ucb_cells(c):
            return arcells(8192 + c * 1024, 8192 + (c + 1) * 1024)

        def usqv(c):
            return ar_bf(12288 + c * 1024, 512)

        def usq_cells(c):
            return arcells(12288 + c * 1024, 12288 + (c + 1) * 1024)

        mean_sb = ar_f32(16384, 512)
        mean_cells = arcells(16384, 18432)
        var_sb = ar_f32(18432, 512)
        var_cells = arcells(18432, 20480)
        mr_sb = ar_f32(20480, 512)
        mr_cells = arcells(20480, 22528)

        def tlnv(s):
            return c1v(s)

        def tln_cells(s):
            return c1_cells(s)

        def stgv(s):
            return ar_f32(s * 4096, 1024)

        def stg_cells(s):
            return arcells(s * 4096, (s + 1) * 4096)

        def obv(s):
            return ar_bf(8192 + s * 2048, 1024)

        def ob_cells(s):
            return arcells(8192 + s * 2048, 8192 + (s + 1) * 2048)

        gkm = ar_f32(12288, 1024)
        gkm_cells = arcells(12288, 16384)
        gkf = ar_f32(16384, 1024)
        gkf_cells = arcells(16384, 20480)

        st_cbw = ar_f32(0, 512)
        st_caw = ar_f32(2048, 512)
        st_cfw = ar_f32(4096, DFF)
        st_vec = ar_f32(15360, 640)
        ST_CELLS = arcells(0, 17920)

        def par_loads(e):
            ins = []
            ins.append(e.dma_start(out=st_cbw[0:31, :], in_=conv_b_w))
            ins.append(e.dma_start(out=st_caw[0:3, :], in_=conv_a_w))
            ins.append(e.dma_start(out=st_cfw[0:3, :], in_=conv_ffn_w))
            ins.append(e.dma_start(out=st_vec[0:8, 0:128], in_=norm_mix_g.rearrange("(k p) -> k p", p=128)))
            ins.append(e.dma_start(out=st_vec[0:8, 128:256], in_=norm_ffn_g.rearrange("(k p) -> k p", p=128)))
            ins.append(e.dma_start(out=st_vec[0:4, 256:384], in_=conv_b_b.rearrange("(c p) -> c p", p=128)))
            ins.append(e.dma_start(out=st_vec[0:4, 384:512], in_=ln_b_g.rearrange("(c p) -> c p", p=128)))
            ins.append(e.dma_start(out=st_vec[0:4, 512:640], in_=ln_b_b.rearrange("(c p) -> c p", p=128)))
            ins.append(e.dma_start(out=gfin[:], in_=norm_final_g.partition_broadcast(128)))
            return ins
        P.dma("sp", par_loads, 9, "d_par", writes=ST_CELLS + ["gfin"])

        P.op("pool", lambda e: e.memset(identf[:], 0.0), writes=["identf"])
        P.op("pool", lambda e: e.affine_select(out=identf[:], in_=identf[:], pattern=[[-1, 128]],
                                               compare_op=ALU.not_equal, fill=1.0, base=0, channel_multiplier=1),
             reads=["identf"], writes=["identf"])
        P.op("dve", lambda e: e.tensor_copy(out=identb[:], in_=identf[:]), reads=["identf"], writes=["identb"])
        P.op("dve", lambda e: e.tensor_scalar(out=identh[:], in0=identf[:], scalar1=0.5, scalar2=None, op0=ALU.mult),
             reads=["identf"], writes=["identh"])

        def par_transposes():
            jobs = [
                (0, [(st_cbw[0:31, c * 128:(c + 1) * 128], 31) for c in range(4)], cbw[:, :, :].rearrange("p c t -> p (c t)"), "cbw"),
                (1, [(st_caw[0:3, c * 128:(c + 1) * 128], 3) for c in range(4)], caw[:, :, :].rearrange("p c t -> p (c t)"), "caw"),
                (2, [(st_cfw[0:3, j * 128:(j + 1) * 128], 3) for j in range(NF)], cfw[:, :, :].rearrange("p j t -> p (j t)"), "cfw"),
                (3, [(st_vec[0:8, 0:128], 8)], gmix[:, :], "gmix"),
                (0, [(st_vec[0:8, 128:256], 8)], gffn[:, :], "gffn"),
                (1, [(st_vec[0:4, 256:384], 4)], cbb[:, :], "cbb"),
                (2, [(st_vec[0:4, 384:512], 4)], lng[:, :], "lng"),
                (3, [(st_vec[0:4, 512:640], 4)], lnb[:, :], "lnb"),
            ]
            for bank, items, dst, name in jobs:
                def tr(e, bank=bank, items=items):
                    last = None
                    off = 0
                    for src_ap, r in items:
                        last = e.transpose(out=pb[bank][:, off:off + r], in_=src_ap, identity=identf[0:r, 0:r])
                        off += r
                    return last
                tot = sum(r for _, r in items)
                P.op("pe", tr, reads=ST_CELLS + ["identf"], writes=[PS(bank)])
                P.op("dve", (lambda e, bank=bank, dst=dst, tot=tot: e.tensor_copy(out=dst, in_=pb[bank][:, 0:tot])),
                     reads=[PS(bank)], writes=[name])

        def mk_consts(e):
            e.memset(onesb[:], 1.0 / 512.0)
            e.memset(epsc[:, 0:1], RMS_EPS)
            return e.memset(epsc[:, 1:2], LN_EPS)
        P.op("pool", mk_consts, writes=["onesb", "epsc"])

        def mk_pads(e):
            e.memset(upad[:], 0.0)
            return e.memset(pp[:], 0.0)
        P.op("pool", mk_pads, writes=[("upad", c) for c in range(4)] + [("pp", s) for s in range(2)])
        par_transposes()

        def load_wout():
            P.dma("pool", lambda e: [e.dma_start(out=wout[:], in_=w_out.rearrange("(c p) d -> p c d", p=128))], 1,
                  "d_wout", writes=["wout"])

        def load_wdn_part(q):
            j0 = 6 * q
            j1 = min(NF, j0 + 6)
            P.dma("pool", (lambda e: [e.dma_start(out=wdn[:, j0:j1, :],
                                                  in_=w_down[j0 * 128:j1 * 128, :].rearrange("(j p) d -> p j d", p=128))]),
                  1, "d_wdn%d" % q, writes=[("wdn", q)])
        WDN_CELLS = [("wdn", q) for q in range(4)]

        win_seq = [(ti, e_) for ti in range(len(tiles)) for e_ in A_ORDER]
        win_state = {"next": 0}

        def prefetch_win():
            n = win_state["next"]
            if n >= len(win_seq):
                return
            win_state["next"] = n + 1
            ti_, e_ = win_seq[n]
            s = n % N_WIN_SLOTS
            if ti_ == 0:
                P.dma("pool", (lambda e, s=s, e_=e_: [e.dma_start(
                    out=winr[:, s, :].rearrange("p (k c) -> p k c", k=KC),
                    in_=w_in[:, e_ * 128:(e_ + 1) * 128].rearrange("(k p) c -> p k c", p=128))]),
                    1, "d_winsw%d" % s, writes=[("winr", s)])
                if len(tiles) > 1:
                    P.dma("sp", (lambda e, s=s, e_=e_: [e.dma_start(out=scr_in[e_], in_=winr[:, s, :])]), 1,
                          "d_wst%d" % s, reads=[("winr", s)], writes=[("scr_in", e_)])
            else:
                P.dma("sp", (lambda e, s=s, e_=e_: [e.dma_start(out=winr[:, s, :], in_=scr_in[e_])]), 1, "d_win%d" % s,
                      reads=[("scr_in", e_)], writes=[("winr", s)])

        gu_total = len(tiles) * NF
        gu_state = {"next": 0}

        def prefetch_gu():
            n = gu_state["next"]
            if n >= gu_total:
                return
            gu_state["next"] = n + 1
            j = n % NF
            s = n % N_GU_SLOTS
            if n < NF:
                def ld(e, s=s, j=j):
                    return [e.dma_start(out=gur[:, s, w_, :].rearrange("p (k c) -> p k c", k=KC),
                                        in_=wsrc[:, j * 128:(j + 1) * 128].rearrange("(k p) c -> p k c", p=128))
                            for w_, wsrc in ((0, w_gate), (1, w_up))]
                P.dma("pool", ld, 2, "d_gusw%d" % s, writes=[("gur", s)])
                if len(tiles) > 1:
                    P.dma("sp", (lambda e, s=s, j=j: [e.dma_start(out=scr_gu[j], in_=gur[:, s, :, :])]), 1,
                          "d_gst%d" % s, reads=[("gur", s)], writes=[("scr_gu", j)])
            else:
                P.dma("sp", (lambda e, s=s, j=j: [e.dma_start(out=gur[:, s, :, :], in_=scr_gu[j])]), 1, "d_gu%d" % s,
                      reads=[("scr_gu", j)], writes=[("gur", s)])


        ctr = {"xa": 0, "stat": 0, "tp": 0, "ev": 0, "diag": 0, "hb": 0, "win": 0, "gu": 0, "c1": 0, "tln": 0}

        def nxt(name, mod):
            v = ctr[name]
            ctr[name] = v + 1
            return v % mod

        def rms_chain(src_ap, src_cells, m, junk_ap, junk_cells, eps_col):
            sc = nxt("stat", NSTAT)
            cell = ("stat", sc)
            P.op("act", (lambda e: e.activation(out=junk_ap, in_=src_ap, func=AF.Square, scale=1.0 / 32.0,
                                                accum_out=stat[0:m, sc, 0:1])),
                 reads=src_cells, writes=list(junk_cells) + [cell])
            P.op("act", (lambda e: e.activation(out=stat[0:m, sc, 1:2], in_=stat[0:m, sc, 0:1], func=AF.Sqrt,
                                                bias=epsc[0:m, eps_col:eps_col + 1], scale=1.0)),
                 reads=[cell, "epsc"], writes=[cell])
            P.op("dve", (lambda e: e.reciprocal(out=stat[0:m, sc, 2:3], in_=stat[0:m, sc, 1:2])),
                 reads=[cell], writes=[cell])
            return stat[0:m, sc, 2:3], cell

        def norm_a(src_ap, src_cells, m, buf_ap, buf_cell):
            rs, rcell = rms_chain(src_ap, src_cells, m, buf_ap, [buf_cell], 0)
            P.op("act", (lambda e: e.activation(out=buf_ap, in_=src_ap, func=AF.Copy, scale=rs)),
                 reads=list(src_cells) + [rcell], writes=[buf_cell])

        def norm_b(buf3, buf_cell, m, dstT, dst_cells, col0, gcol, gname):
            tb = nxt("tp", 2)

            def tr(e):
                last = None
                for k in range(KC):
                    last = e.transpose(out=tp[tb][:, k, 0:m], in_=buf3[:, k, :], identity=identb[0:m, 0:m])
                return last
            P.op("pe", tr, reads=[buf_cell, "identb"], writes=[PS(6 + tb)])
            P.op("dve", (lambda e: e.tensor_tensor(out=dstT[:, :, col0:col0 + m], in0=tp[tb][:, :, 0:m],
                                                   in1=gcol[:, 0:KC].unsqueeze(2).broadcast_to([128, KC, m]),
                                                   op=ALU.mult)),
                 reads=[PS(6 + tb), gname], writes=dst_cells)

        def s0_geom(ti):
            t0, T = tiles[ti]
            a0 = max(0, t0 - HALO)
            a1 = min(S, t0 + T + HALO)
            nA = a1 - a0
            return a0, nA, (nA + 127) // 128

        def s0_load(ti, c):
            a0, nA, nch = s0_geom(ti)
            if c >= nch:
                return
            m = min(128, nA - 128 * c)
            s = c % 2
            r0 = a0 + 128 * c
            P.dma("sp", (lambda e, s=s, r0=r0, m=m: [e.dma_start(out=xa[0:m, s, :], in_=x[r0:r0 + m, :])]), 1,
                  "d_xa%d" % s, writes=[("xa", s)])

        def s0_norm(ti, c):
            a0, nA, nch = s0_geom(ti)
            if c >= nch:
                return
            m = min(128, nA - 128 * c)
            s = c % 2
            norm_a(xa[0:m, s, :], [("xa", s)], m, hbs[0:m, c, :], ("hbs", c))

        def stage_S0b(ti, c):
            a0, nA, nch = s0_geom(ti)
            if c >= nch:
                return
            m = min(128, nA - 128 * c)
            norm_b(hbs[0:m, c, :].rearrange("p (k c) -> p k c", k=KC), ("hbs", c), m, hT, [("hT", c)], 128 * c,
                   gmix, "gmix")

        def stage_mixer(ti):
            t0, T = tiles[ti]
            a0 = max(0, t0 - HALO)
            a1 = min(S, t0 + T + HALO)
            nA = a1 - a0
            b0 = max(0, t0 - 1)
            b1 = min(S, t0 + T + 1)
            nB = b1 - b0
            oB = b0 - a0
            ncA = (nA + 127) // 128
            ntc = (nB + 127) // 128

            for tc in range(ntc):
                m = min(128, nB - 128 * tc)
                r0 = b0 + 128 * tc
                P.dma("pool", (lambda e, tc=tc, r0=r0, m=m: [e.dma_start(out=x1[0:m, tc, :], in_=x[r0:r0 + m, :])]), 1,
                      "d_x1_%d" % tc, writes=[("x1", tc)])

            hT_cells = [("hT", c) for c in range(ncA)]
            abank = {"n": 0}
            evslot = {}
            ctslot = {}
            ppslot = {}

            if nA < WIN:
                P.op("pool", (lambda e: e.memset(upad[:, :, 15 + nA:UPW], 0.0)),
                     writes=[("upad", c) for c in range(4)])

            def a_group(e_):
                ws = nxt("win", N_WIN_SLOTS)
                bank = abank["n"] % 4
                abank["n"] += 1

                def mmA(e, ws=ws, bank=bank):
                    last = None
                    for k in range(KC):
                        last = e.matmul(pb[bank][:, 0:nA], lhsT=winr[:, ws, k * 128:(k + 1) * 128], rhs=hT[:, k, 0:nA],
                                        start=(k == 0), stop=(k == KC - 1))
                    return last
                P.op("pe", mmA, reads=[("winr", ws)] + hT_cells, writes=[PS(bank)])
                prefetch_win()
                if ti == 0 and e_ == A_ORDER[-5]:
                    for _ in range(N_GU_SLOTS):
                        prefetch_gu()
                    load_wout()
                    for q in range(4):
                        load_wdn_part(q)
                kind, c = e_ // 4, e_ % 4
                if kind == 4:
                    s = nxt("ev", 3)
                    evslot[("g", c)] = s
                    P.op("act", (lambda e, s=s, bank=bank: e.activation(out=ev[:, s, 0:nA], in_=pb[bank][:, 0:nA],
                                                                         func=AF.Tanh, scale=0.5)),
                         reads=[PS(bank)], writes=[("ev", s)])
                elif kind == 3:
                    s = evslot[("g", c)]
                    P.op("dve", (lambda e, s=s, bank=bank, c=c: e.scalar_tensor_tensor(
                        out=upad[:, c, 15:15 + nA], in0=ev[:, s, 0:nA], scalar=1.0, in1=pb[bank][:, 0:nA],
                        op0=ALU.add, op1=ALU.mult)),
                        reads=[("ev", s), PS(bank)], writes=[("upad", c)])
                elif kind == 0:
                    s = nxt("ev", 3)
                    evslot[("h", c)] = s
                    P.op("act", (lambda e, s=s, bank=bank: e.activation(out=ev[:, s, 0:nA], in_=pb[bank][:, 0:nA],
                                                                         func=AF.Copy)),
                         reads=[PS(bank)], writes=[("ev", s)])
                elif kind == 2:
                    s = evslot[("h", c)]
                    ps_ = c % 2
                    cs = c
                    ppslot[c] = ps_
                    ctslot[c] = cs
                    P.op("dve", (lambda e, s=s, bank=bank, ps_=ps_: e.tensor_tensor(
                        out=pp[:, ps_, 1:1 + nA], in0=pb[bank][:, 0:nA], in1=ev[:, s, 0:nA], op=ALU.mult)),
                        reads=[("ev", s), PS(bank)], writes=[("pp", ps_)])
                    if nA < WIN:
                        P.op("pool", (lambda e, ps_=ps_: e.memset(pp[:, ps_, 1 + nA:PPW], 0.0)), writes=[("pp", ps_)])
                    P.op("act", (lambda e, ps_=ps_, cs=cs, c=c: e.activation(
                        out=ct[:, cs, 0:nB], in_=pp[:, ps_, 1 + oB:1 + oB + nB], func=AF.Copy, scale=caw[:, c, 1:2])),
                        reads=[("pp", ps_), "caw"], writes=[("ct", cs)])
                    P.op("dve", (lambda e, ps_=ps_, cs=cs, c=c: e.scalar_tensor_tensor(
                        out=ct[:, cs, 0:nB], in0=pp[:, ps_, oB:oB + nB], scalar=caw[:, c, 0:1], in1=ct[:, cs, 0:nB],
                        op0=ALU.mult, op1=ALU.add)), reads=[("pp", ps_), ("ct", cs), "caw"], writes=[("ct", cs)])
                    P.op("dve", (lambda e, ps_=ps_, cs=cs, c=c: e.scalar_tensor_tensor(
                        out=ct[:, cs, 0:nB], in0=pp[:, ps_, 2 + oB:2 + oB + nB], scalar=caw[:, c, 2:3],
                        in1=ct[:, cs, 0:nB], op0=ALU.mult, op1=ALU.add)),
                        reads=[("pp", ps_), ("ct", cs), "caw"], writes=[("ct", cs)])
                elif kind == 1:
                    cs = ctslot[c]
                    P.op("dve", (lambda e, cs=cs, bank=bank, c=c: e.tensor_tensor(
                        out=yT[:, c, 0:nB], in0=pb[bank][:, oB:oB + nB], in1=ct[:, cs, 0:nB], op=ALU.mult)),
                        reads=[("ct", cs), PS(bank)], writes=[("yT", c)])

            diag_q = []

            def gen_diag(idx):
                c, kb = idx // 4, idx % 4
                k0 = 8 * kb
                nt = min(8, 31 - k0)
                ds = nxt("diag", N_DIAG_BLK)
                P.op(DIAG_ENGINE, (lambda e, ds=ds, c=c, k0=k0, nt=nt: e.tensor_tensor(
                    out=diag[:, ds, 0:nt, :], in0=identh[:].unsqueeze(1).broadcast_to([128, nt, 128]),
                    in1=cbw[:, c, k0:k0 + nt].unsqueeze(2).broadcast_to([128, nt, 128]), op=ALU.mult)),
                    reads=["identh", "cbw"], writes=[("diag", ds)])
                diag_q.append(ds)

            diag_state = {"gen": 0}

            def ensure_diag(upto):
                while diag_state["gen"] <= upto and diag_state["gen"] < 16:
                    gen_diag(diag_state["gen"])
                    diag_state["gen"] += 1

            def conv_chunk(c):
                bank = 4 + (c % 2)
                for kb in range(4):
                    k0 = 8 * kb
                    nt = min(8, 31 - k0)
                    idx = 4 * c + kb
                    ensure_diag(idx)
                    ds = diag_q[idx]

                    def mmC(e, ds=ds, c=c, k0=k0, nt=nt, bank=bank):
                        last = None
                        for t in range(nt):
                            k = k0 + t
                            last = e.matmul(pb[bank][:, 0:nB], lhsT=diag[:, ds, t, :],
                                            rhs=upad[:, c, oB + k:oB + k + nB], start=(k == 0), stop=(k == 30))
                        return last
                    P.op("pe", mmC, reads=[("diag", ds), ("upad", c)], writes=[PS(bank)])
                    ensure_diag(min(15, idx + N_DIAG_BLK - 1))
                P.op("act", (lambda e, c=c, bank=bank: e.activation(out=ucbv(c)[:, 0:nB], in_=pb[bank][:, 0:nB],
                                                                     func=AF.Identity, bias=cbb[:, c:c + 1], scale=1.0)),
                     reads=[PS(bank), "cbb"], writes=ucb_cells(c))
                P.op("act", (lambda e, c=c, bank=bank: e.activation(out=usqv(c)[:, 0:nB], in_=pb[bank][:, 0:nB],
                                                                     func=AF.Square, bias=cbb[:, c:c + 1], scale=1.0)),
                     reads=[PS(bank), "cbb"], writes=usq_cells(c))
                P.op("act", (lambda e, c=c, bank=bank: e.activation(out=ucv(c)[:, 0:nB], in_=pb[bank][:, 0:nB],
                                                                     func=AF.Identity, bias=cbb[:, c:c + 1], scale=1.0)),
                     reads=[PS(bank), "cbb"], writes=uc_cells(c))

            def stat_chunk(c):
                P.op("pe", (lambda e, c=c: e.matmul(stat_ps[0][:, 0:nB], lhsT=onesb[:], rhs=ucbv(c)[:, 0:nB],
                                                     start=(c == 0), stop=(c == 3))),
                     reads=["onesb"] + ucb_cells(c), writes=[PS(6)])
                P.op("pe", (lambda e, c=c: e.matmul(stat_ps[1][:, 0:nB], lhsT=onesb[:], rhs=usqv(c)[:, 0:nB],
                                                     start=(c == 0), stop=(c == 3))),
                     reads=["onesb"] + usq_cells(c), writes=[PS(7)])

            ensure_diag(N_DIAG_BLK - 2)
            for e_ in A_ORDER[:8]:
                a_group(e_)
            conv_chunk(0)
            conv_chunk(1)
            stat_chunk(0)
            conv_chunk(2)
            stat_chunk(1)
            conv_chunk(3)
            stat_chunk(2)
            stat_chunk(3)

            L = []
            L.append(lambda: P.op("act", (lambda e: e.activation(out=mean_sb[:, 0:nB], in_=stat_ps[0][:, 0:nB],
                                                                 func=AF.Copy)), reads=[PS(6)], writes=mean_cells))
            L.append(lambda: P.op("act", (lambda e: e.activation(out=var_sb[:, 0:nB], in_=stat_ps[0][:, 0:nB],
                                                                 func=AF.Square)), reads=[PS(6)], writes=var_cells))
            L.append(lambda: P.op("dve", (lambda e: e.tensor_tensor(out=var_sb[:, 0:nB], in0=stat_ps[1][:, 0:nB],
                                                                    in1=var_sb[:, 0:nB], op=ALU.subtract)),
                                  reads=[PS(7)] + var_cells, writes=var_cells))
            L.append(lambda: P.op("act", (lambda e: e.activation(out=var_sb[:, 0:nB], in_=var_sb[:, 0:nB], func=AF.Sqrt,
                                                                 bias=epsc[:, 1:2], scale=1.0)),
                                  reads=var_cells + ["epsc"], writes=var_cells))
            L.append(lambda: P.op("dve", (lambda e: e.reciprocal(out=var_sb[:, 0:nB], in_=var_sb[:, 0:nB])),
                                  reads=var_cells, writes=var_cells))
            LN_mul, LN_sub, LN_silu = [], [], []
            for c in range(4):
                eng_c = "dve" if c % 2 == 0 else LN_ENGINE
                LN_sub.append(lambda c=c, eng_c=eng_c: P.op(eng_c, (lambda e: e.tensor_tensor(
                    out=ucv(c)[:, 0:nB], in0=ucv(c)[:, 0:nB], in1=mean_sb[:, 0:nB], op=ALU.subtract)),
                    reads=uc_cells(c) + mean_cells, writes=uc_cells(c)))
                LN_mul.append(lambda c=c, eng_c=eng_c: P.op(eng_c, (lambda e: e.tensor_tensor(
                    out=ucv(c)[:, 0:nB], in0=ucv(c)[:, 0:nB], in1=var_sb[:, 0:nB], op=ALU.mult)),
                    reads=uc_cells(c) + var_cells, writes=uc_cells(c)))
                LN_silu.append(lambda c=c: P.op("act", (lambda e: e.activation(
                    out=yT[:, 4 + c, 0:nB], in_=ucv(c)[:, 0:nB], func=AF.Silu, scale=lng[:, c:c + 1],
                    bias=lnb[:, c:c + 1])), reads=uc_cells(c) + ["lng", "lnb"], writes=[("yT", 4 + c)]))
            sched = {
                0: [L[0], L[1]],
                1: [L[2], LN_sub[0], LN_sub[1]],
                2: [L[3], LN_sub[2], LN_sub[3]],
                4: [L[4]],
                5: [LN_mul[0], LN_mul[1]],
                6: [LN_mul[2], LN_mul[3]],
                8: [LN_silu[0], LN_silu[1]],
                9: [LN_silu[2], LN_silu[3]],
            }
            for gi, e_ in enumerate(A_ORDER[8:]):
                a_group(e_)
                for th in sched.get(gi, []):
                    th()
                if gi == 9:
                    act_warm(AF.Sqrt)

            yT_cells = [("yT", c) for c in range(8)]
            obank = {"n": 0}
            pend_b = None
            for tc in range(ntc):
                m = min(128, nB - 128 * tc)
                for dh in range(2):
                    bank = 4 + (obank["n"] % 2)
                    obank["n"] += 1

                    def mmO(e, tc=tc, m=m, dh=dh, bank=bank):
                        last = None
                        for cc in range(KC):
                            last = e.matmul(pb[bank][0:m, :], lhsT=yT[:, cc, 128 * tc:128 * tc + m],
                                            rhs=wout[:, cc, dh * 512:(dh + 1) * 512], start=(cc == 0), stop=(cc == KC - 1))
                        return last
                    P.op("pe", mmO, reads=yT_cells + ["wout"], writes=[PS(bank)])
                    P.op("dve", (lambda e, tc=tc, m=m, dh=dh, bank=bank: e.tensor_tensor(
                        out=x1[0:m, tc, dh * 512:(dh + 1) * 512], in0=pb[bank][0:m, :],
                        in1=x1[0:m, tc, dh * 512:(dh + 1) * 512], op=ALU.add)),
                        reads=[PS(bank), ("x1", tc)], writes=[("x1", tc)])
                hs = nxt("hb", 3)
                norm_a(x1[0:m, tc, :], [("x1", tc)], m, hbf[0:m, hs, :], ("hbf", hs))
                if pend_b is not None:
                    pend_b()

                def pb_(tc=tc, m=m, hs=hs):
                    norm_b(hbf[0:m, hs, :].rearrange("p (k c) -> p k c", k=KC), ("hbf", hs), m, h2T, [("h2T", tc)],
                           128 * tc, gffn, "gffn")
                pend_b = pb_
            return nB, ntc, b0, pend_b

        def stage_ffn(ti, nB, ntc, b0, hoist, pend_b):
            t0, T = tiles[ti]
            h2T_cells = [("h2T", tc) for tc in range(ntc)]
            pending_mul = None
            tails = []
            evq = []
            if ntc <= 1:
                pend_b()
            act_warm(AF.Silu)
            for j in range(NF):
                gs = nxt("gu", N_GU_SLOTS)
                if j < 18:
                    gb = (0, 1, 2)[j % 3]
                    vb = (4, 5, 3)[j % 3]
                else:
                    gb = (0, 1, 6)[j % 3]
                    vb = (4, 5, 7)[j % 3]

                def mk_mm(w_, bank, c0, c1, gs=gs):
                    def mm(e):
                        last = None
                        for k in range(KC):
                            last = e.matmul(pbv(bank)[:, c0:c1], lhsT=gur[:, gs, w_, k * 128:(k + 1) * 128],
                                            rhs=h2T[:, k, c0:c1], start=(k == 0), stop=(k == KC - 1))
                        return last
                    return mm
                if j < 2 and ntc > 1:
                    c_head = 128 * max(1, ntc - 2)
                    pieces = [(0, c_head)]
                    if ntc > 2:
                        pieces.append((c_head, 128 * (ntc - 1)))
                    pieces.append((128 * (ntc - 1), nB))
                    hd = h2T_cells[:max(1, ntc - 2)]
                    P.op("pe", mk_mm(0, gb, 0, c_head), reads=[("gur", gs)] + hd, writes=[PS(gb)])
                    P.op("pe", mk_mm(1, vb, 0, c_head), reads=[("gur", gs)] + hd, writes=[PS(vb)])
                    tails.append((gs, gb, vb))
                    if j == 1:
                        pend_b()
                        for pi_, (c0_, c1_) in enumerate(pieces[1:]):
                            need = h2T_cells[:(c1_ + 127) // 128]
                            for (gs_, gb_, vb_) in tails:
                                P.op("pe", mk_mm(0, gb_, c0_, c1_, gs_), reads=[("gur", gs_)] + need, writes=[PS(gb_)])
                                P.op("pe", mk_mm(1, vb_, c0_, c1_, gs_), reads=[("gur", gs_)] + need, writes=[PS(vb_)])
                else:
                    P.op("pe", mk_mm(0, gb, 0, nB), reads=[("gur", gs)] + h2T_cells, writes=[PS(gb)])
                    P.op("pe", mk_mm(1, vb, 0, nB), reads=[("gur", gs)] + h2T_cells, writes=[PS(vb)])
                if j < 2 and ntc > 1:
                    if j == 1:
                        prefetch_gu()
                        prefetch_gu()
                else:
                    prefetch_gu()
                def evac(j=j, gb=gb, vb=vb):
                    nonlocal pending_mul
                    cs = nxt("c1", 2)
                    cc_ = c1_cells(cs)
                    P.op("act", (lambda e, cs=cs, gb=gb, j=j: e.activation(out=c1v(cs)[:, 0:nB], in_=pbv(gb)[:, 0:nB],
                                                                            func=AF.Copy, scale=cfw[:, j, 1:2])),
                         reads=[PS(gb), "cfw"], writes=cc_)
                    P.op("dve", (lambda e, cs=cs, gb=gb, j=j: e.scalar_tensor_tensor(
                        out=c1v(cs)[:, 1:nB], in0=pbv(gb)[:, 0:nB - 1], scalar=cfw[:, j, 0:1], in1=c1v(cs)[:, 1:nB],
                        op0=ALU.mult, op1=ALU.add)), reads=[PS(gb), "cfw"] + cc_, writes=cc_)
                    if pending_mul is not None:
                        pending_mul()
                    P.op("dve", (lambda e, cs=cs, gb=gb, j=j: e.scalar_tensor_tensor(
                        out=c1v(cs)[:, 0:nB - 1], in0=pbv(gb)[:, 1:nB], scalar=cfw[:, j, 2:3], in1=c1v(cs)[:, 0:nB - 1],
                        op0=ALU.mult, op1=ALU.add)), reads=[PS(gb), "cfw"] + cc_, writes=cc_)
                    P.op("act", (lambda e, cs=cs: e.activation(out=c1v(cs)[:, 0:nB], in_=c1v(cs)[:, 0:nB], func=AF.Silu)),
                         reads=cc_, writes=cc_)

                    def mul(cs=cs, vb=vb, j=j, cc_=cc_):
                        P.op("dve", (lambda e: e.tensor_tensor(out=aT(j)[:, 0:nB], in0=pbv(vb)[:, 0:nB],
                                                               in1=c1v(cs)[:, 0:nB], op=ALU.mult)),
                             reads=[PS(vb)] + cc_, writes=aT_cells(j))
                    pending_mul = mul
                if j < 2 and ntc > 1:
                    evq.append(evac)
                    if j == 1:
                        for ev_ in evq:
                            ev_()
                else:
                    evac()
            pending_mul()
            act_warm(AF.Sqrt)

            aT_all = [c for j in range(NF) for c in aT_cells(j)]
            dbank = {"n": 0}
            if hoist is not None:
                s0_load(hoist, 0)
                s0_load(hoist, 1)
            for tc in range(ntc):
                m = min(128, nB - 128 * tc)
                for dh in range(2):
                    bank = 2 + (dbank["n"] % 2)
                    dbank["n"] += 1

                    def mmD(e, tc=tc, m=m, dh=dh, bank=bank):
                        last = None
                        for j in range(NF):
                            last = e.matmul(pb[bank][0:m, :], lhsT=aT(j)[:, 128 * tc:128 * tc + m],
                                            rhs=wdn[:, j, dh * 512:(dh + 1) * 512], start=(j == 0), stop=(j == NF - 1))
                        return last
                    if dbank["n"] == 1:
                        def mmD1(e, tc=tc, m=m, dh=dh, bank=bank):
                            last = None
                            for j in range(NF - 4):
                                last = e.matmul(pb[bank][0:m, :], lhsT=aT(j)[:, 128 * tc:128 * tc + m],
                                                rhs=wdn[:, j, dh * 512:(dh + 1) * 512], start=(j == 0), stop=False)
                            return last

                        def mmD2(e, tc=tc, m=m, dh=dh, bank=bank):
                            last = None
                            for j in range(NF - 4, NF):
                                last = e.matmul(pb[bank][0:m, :], lhsT=aT(j)[:, 128 * tc:128 * tc + m],
                                                rhs=wdn[:, j, dh * 512:(dh + 1) * 512], start=False, stop=(j == NF - 1))
                            return last
                        head_cells = [c for j in range(NF - 4) for c in aT_cells(j)]
                        P.op("pe", mmD1, reads=head_cells + WDN_CELLS, writes=[PS(bank)])
                        P.op("pe", mmD2, reads=aT_all + WDN_CELLS, writes=[PS(bank)])
                    else:
                        P.op("pe", mmD, reads=aT_all + WDN_CELLS, writes=[PS(bank)])
                    if hoist is not None:
                        g_ = dbank["n"] - 1
                        if g_ == 0:
                            s0_norm(hoist, 0)
                            s0_load(hoist, 2)
                        elif g_ == 1:
                            s0_norm(hoist, 1)
                            s0_load(hoist, 3)
                        elif g_ == 2:
                            stage_S0b(hoist, 0)
                            s0_norm(hoist, 2)
                        elif g_ == 3:
                            stage_S0b(hoist, 1)
                            s0_norm(hoist, 3)
                        elif g_ == 4:
                            stage_S0b(hoist, 2)
                        elif g_ == 5:
                            stage_S0b(hoist, 3)
                    P.op("dve", (lambda e, tc=tc, m=m, dh=dh, bank=bank: e.tensor_tensor(
                        out=x1[0:m, tc, dh * 512:(dh + 1) * 512], in0=pb[bank][0:m, :],
                        in1=x1[0:m, tc, dh * 512:(dh + 1) * 512], op=ALU.add)),
                        reads=[PS(bank), ("x1", tc)], writes=[("x1", tc)])
                hs = nxt("hb", 3)
                rs, rcell = rms_chain(x1[0:m, tc, :], [("x1", tc)], m, hbf[0:m, hs, :], [("hbf", hs)], 0)
                P.op("dve", (lambda e, tc=tc, m=m, rs=rs: e.scalar_tensor_tensor(
                    out=x1[0:m, tc, :], in0=x1[0:m, tc, :], scalar=rs, in1=gfin[0:m, :], op0=ALU.mult, op1=ALU.mult)),
                    reads=[("x1", tc), rcell, "gfin"], writes=[("x1", tc)])
                r_lo = max(0, (t0 - b0) - 128 * tc)
                r_hi = min(m, (t0 + T - b0) - 128 * tc)
                if r_lo < r_hi:
                    g0 = b0 + 128 * tc
                    tok = P.dma("pool", (lambda e, tc=tc, r_lo=r_lo, r_hi=r_hi, g0=g0: [e.dma_start(
                        out=y[g0 + r_lo:g0 + r_hi, :], in_=x1[r_lo:r_hi, tc, :])]), 1, "d_x1_%d" % tc,
                        reads=[("x1", tc)])
                    P.final_tokens.append(tok)

        s0_load(0, 0)
        s0_load(0, 1)
        s0_norm(0, 0)
        s0_load(0, 2)
        s0_norm(0, 1)
        s0_load(0, 3)
        for _ in range(N_WIN_SLOTS):
            prefetch_win()
        stage_S0b(0, 0)
        stage_S0b(0, 1)
        s0_norm(0, 2)
        stage_S0b(0, 2)
        s0_norm(0, 3)
        stage_S0b(0, 3)
        for ti in range(len(tiles)):
            nB, ntc, b0, pend_b = stage_mixer(ti)
            hoist = ti + 1 if ti + 1 < len(tiles) else None
            stage_ffn(ti, nB, ntc, b0, hoist, pend_b)

        P.emit(nc, es)
    return nc


_PARAM_KEYS = ["norm_mix_g", "w_in", "conv_a_w", "conv_b_w", "conv_b_b", "ln_b_g", "ln_b_b", "w_out",
               "norm_ffn_g", "w_gate", "w_up", "conv_ffn_w", "w_down", "norm_final_g"]


def _prep_params(inputs):
    out = {}
    for k in _PARAM_KEYS:
        a = np.asarray(inputs[k], dtype=np.float32)
        if k != "norm_final_g":
            a = a[0]
        out[k] = np.ascontiguousarray(a)
    return out


def kernel(**inputs):
    x = np.asarray(inputs["x"], dtype=np.float32)
    B, S, _ = x.shape
    params = _prep_params(inputs)
    nc = build(S)
    in_maps = []
    for b in range(B):
        m = dict(params)
        m["x"] = np.ascontiguousarray(x[b])
        in_maps.append(m)
    res = run_bass_kernel_spmd(nc, in_maps, core_ids=list(range(B)))
    return np.stack([np.asarray(r["y"], dtype=np.float32) for r in res.results], axis=0)
```
